# Optimizing a Trainium2 kernel written in Bass

```python
import math
import jax
import jax.numpy as jnp
from jax import lax
import numpy as np

D_MODEL = 2048
BATCH = 4
SEQ = 4096
DEPTH = 1

N_META = 16
BLOCK = 128
PAD = (-N_META) % BLOCK
N_DIFF_HEADS = 8
DIFF_QK_DIM = 64
DIFF_V_DIM = 2 * DIFF_QK_DIM
N_SB_HEADS = 8
SB_HEAD_DIM = 128
DIFF_Q_W = N_DIFF_HEADS * 2 * DIFF_QK_DIM
DIFF_K_W = N_DIFF_HEADS * 2 * DIFF_QK_DIM
DIFF_V_W = N_DIFF_HEADS * DIFF_V_DIM
SB_W = N_SB_HEADS * SB_HEAD_DIM
MIX_W = DIFF_V_W + SB_W
IN_W = DIFF_Q_W + DIFF_K_W + DIFF_V_W + 3 * SB_W
D_FF = 4 * D_MODEL
ROPE_THETA = 10000.0
EPS = 1e-6
NEG_INF = -1e30

kernel_name = "hymba_diff_stickbreaking_hybrid"


def _rmsnorm(x, g):
    xf = x.astype(jnp.float32)
    y = xf * lax.rsqrt(jnp.mean(xf * xf, axis=-1, keepdims=True) + EPS)
    return (y * g.astype(jnp.float32)).astype(x.dtype)


def _rope_tables(length, dim):
    pos = jnp.arange(length, dtype=jnp.float32) - PAD
    inv = ROPE_THETA ** (-jnp.arange(0, dim, 2, dtype=jnp.float32) / dim)
    ang = pos[:, None] * inv[None, :]
    return jnp.cos(ang), jnp.sin(ang)


def _rope(x, cos, sin):
    half = x.shape[-1] // 2
    shape = (1, cos.shape[0]) + (1,) * (x.ndim - 3) + (half,)
    c = cos.reshape(shape).astype(x.dtype)
    s = sin.reshape(shape).astype(x.dtype)
    x1, x2 = x[..., :half], x[..., half:]
    return jnp.concatenate([x1 * c - x2 * s, x2 * c + x1 * s], axis=-1)


def _to_blocks(q):
    b, h, l, d = q.shape
    return q.reshape(b, h, l // BLOCK, BLOCK, d).transpose(2, 0, 1, 3, 4)


def _from_blocks(o):
    nb, b, h, blk, e = o.shape
    return o.transpose(1, 0, 3, 2, 4).reshape(b, nb * blk, h, e)


def _diff_attention(q1, q2, k1, k2, v, lam):
    length, d = q1.shape[2], q1.shape[3]
    kpos = jnp.arange(length)
    scale = d ** -0.5

    def one_block(args):
        i, q1b, q2b = args
        qpos = i * BLOCK + jnp.arange(BLOCK)
        mask = (kpos[None, :] <= qpos[:, None]) & (kpos[None, :] >= PAD)

        def probs(qb, k):
            s = jnp.einsum('bhqd,bhkd->bhqk', qb, k).astype(jnp.float32) * scale
            return jax.nn.softmax(jnp.where(mask, s, NEG_INF), axis=-1)

        w = probs(q1b, k1) - lam * probs(q2b, k2)
        return jnp.einsum('bhqk,bhkv->bhqv', w.astype(v.dtype), v)

    nb = length // BLOCK
    out = lax.map(one_block, (jnp.arange(nb), _to_blocks(q1), _to_blocks(q2)))
    return _from_blocks(out)


def _stick_breaking_attention(q, k, v):
    length, d = q.shape[2], q.shape[3]
    kpos = jnp.arange(length)
    scale = d ** -0.5

    def one_block(args):
        i, qb = args
        qpos = i * BLOCK + jnp.arange(BLOCK)
        strict = (kpos[None, :] < qpos[:, None]) & (kpos[None, :] >= PAD)
        z = jnp.einsum('bhqd,bhkd->bhqk', qb, k).astype(jnp.float32) * scale
        log_not = jnp.where(strict, jax.nn.log_sigmoid(-z), 0.0)
        suffix = lax.cumsum(log_not, axis=3, reverse=True) - log_not
        log_a = jax.nn.log_sigmoid(z) + suffix
        a = jnp.where(strict, jnp.exp(log_a), 0.0)
        return jnp.einsum('bhqk,bhkd->bhqd', a.astype(v.dtype), v)

    nb = length // BLOCK
    out = lax.map(one_block, (jnp.arange(nb), _to_blocks(q)))
    return _from_blocks(out)


def setup_inputs(seed: int = 0) -> dict:
    key = jax.random.key(seed)
    ks = jax.random.split(key, 20)
    f32 = jnp.float32

    def nrm(k, shape, scale):
        return jax.random.normal(k, shape, f32) * scale

    def gain(k, shape):
        return 1.0 + 0.02 * jax.random.normal(k, shape, f32)

    return {
        "x": nrm(ks[0], (BATCH, SEQ, D_MODEL), 1.0),
        "meta_tokens": nrm(ks[1], (N_META, D_MODEL), 1.0),
        "g_mix": gain(ks[2], (DEPTH, D_MODEL)),
        "w_in": nrm(ks[3], (DEPTH, D_MODEL, IN_W), D_MODEL ** -0.5),
        "q_norm_g": gain(ks[4], (DEPTH, DIFF_QK_DIM)),
        "k_norm_g": gain(ks[5], (DEPTH, DIFF_QK_DIM)),
        "lambda_q1": nrm(ks[6], (DEPTH, DIFF_QK_DIM), 0.1),
        "lambda_k1": nrm(ks[7], (DEPTH, DIFF_QK_DIM), 0.1),
        "lambda_q2": nrm(ks[8], (DEPTH, DIFF_QK_DIM), 0.1),
        "lambda_k2": nrm(ks[9], (DEPTH, DIFF_QK_DIM), 0.1),
        "diff_out_g": gain(ks[10], (DEPTH, DIFF_V_DIM)),
        "sb_out_g": gain(ks[11], (DEPTH, SB_HEAD_DIM)),
        "w_out": nrm(ks[12], (DEPTH, MIX_W, D_MODEL), MIX_W ** -0.5),
        "g_mlp": gain(ks[13], (DEPTH, D_MODEL)),
        "w_up": nrm(ks[14], (DEPTH, D_MODEL, D_FF), D_MODEL ** -0.5),
        "w_down": nrm(ks[15], (DEPTH, D_FF, D_MODEL), D_FF ** -0.5),
    }


def reference(x, meta_tokens, g_mix, w_in, q_norm_g, k_norm_g, lambda_q1, lambda_k1,
              lambda_q2, lambda_k2, diff_out_g, sb_out_g, w_out, g_mlp, w_up, w_down):
    b, seq, dm = x.shape
    dummy = jnp.zeros((b, PAD, dm), x.dtype)
    meta = jnp.broadcast_to(meta_tokens.astype(x.dtype)[None], (b, N_META, dm))
    h = jnp.concatenate([dummy, meta, x], axis=1)
    length = h.shape[1]
    cos, sin = _rope_tables(length, DIFF_QK_DIM)
    split_at = list(np.cumsum([DIFF_Q_W, DIFF_K_W, DIFF_V_W, SB_W, SB_W]))

    for l in range(DEPTH):
        lambda_init = 0.8 - 0.6 * math.exp(-0.3 * l)
        u = _rmsnorm(h, g_mix[l])
        proj = jnp.einsum('bld,de->ble', u, w_in[l])
        dq, dk, dv, sq, sk, sv = jnp.split(proj, split_at, axis=-1)

        dq = dq.reshape(b, length, N_DIFF_HEADS, 2, DIFF_QK_DIM)
        dk = dk.reshape(b, length, N_DIFF_HEADS, 2, DIFF_QK_DIM)
        dq = _rope(_rmsnorm(dq, q_norm_g[l]), cos, sin).transpose(0, 2, 3, 1, 4)
        dk = _rope(_rmsnorm(dk, k_norm_g[l]), cos, sin).transpose(0, 2, 3, 1, 4)
        dv = dv.reshape(b, length, N_DIFF_HEADS, DIFF_V_DIM).transpose(0, 2, 1, 3)
        lam = (jnp.exp(jnp.sum(lambda_q1[l] * lambda_k1[l]).astype(jnp.float32))
               - jnp.exp(jnp.sum(lambda_q2[l] * lambda_k2[l]).astype(jnp.float32))
               + lambda_init)
        o_diff = _diff_attention(dq[:, :, 0], dq[:, :, 1], dk[:, :, 0], dk[:, :, 1], dv, lam)
        o_diff = (_rmsnorm(o_diff, diff_out_g[l]) * (1.0 - lambda_init)).reshape(b, length, DIFF_V_W)

        sq = sq.reshape(b, length, N_SB_HEADS, SB_HEAD_DIM).transpose(0, 2, 1, 3)
        sk = sk.reshape(b, length, N_SB_HEADS, SB_HEAD_DIM).transpose(0, 2, 1, 3)
        sv = sv.reshape(b, length, N_SB_HEADS, SB_HEAD_DIM).transpose(0, 2, 1, 3)
        o_sb = _stick_breaking_attention(sq, sk, sv)
        o_sb = _rmsnorm(o_sb, sb_out_g[l]).reshape(b, length, SB_W)

        mixed = jnp.concatenate([o_diff, o_sb], axis=-1)
        h = h + jnp.einsum('ble,ed->bld', mixed, w_out[l])

        m = _rmsnorm(h, g_mlp[l])
        hid = jnp.square(jax.nn.relu(jnp.einsum('bld,df->blf', m, w_up[l])))
        h = h + jnp.einsum('blf,fd->bld', hid, w_down[l])

    return h[:, PAD + N_META:]
```

```python
import contextlib
import numpy as np
import concourse.bass as bass
import concourse.mybir as mybir
from concourse.bass_utils import run_bass_kernel_spmd

F32 = mybir.dt.float32
BF16 = mybir.dt.bfloat16
AF = mybir.ActivationFunctionType
ALU = mybir.AluOpType
AX = mybir.AxisListType

D = 2048
L = 4224
NB = 33
TOWN = 2048
NOB = 16
DFF = 8192
EPS = 1e-6
PAD = 112
LAMBDA_INIT = 0.8 - 0.6 * 1.0

COMPUTE = ("tensor", "vector", "scalar", "gpsimd")


class _Op:
    __slots__ = ("eng", "fn", "reads", "writes", "dma", "chan", "waits", "inc", "tick", "idx")

    def __init__(self, eng, fn, reads, writes, dma, chan):
        self.eng = eng
        self.fn = fn
        self.reads = reads
        self.writes = writes
        self.dma = dma
        self.chan = chan
        self.waits = set()
        self.inc = False
        self.tick = None


class Prog:
    def __init__(self, nc):
        self.nc = nc
        self.ops = []
        self.last_writer = {}
        self.readers = {}
        self.eng_count = {e: 0 for e in COMPUTE}
        self.chan_count = {}

    def add(self, eng, fn, reads=(), writes=(), dma=False, chan=None):
        op = _Op(eng, fn, tuple(reads), tuple(writes), dma, chan)
        op.idx = len(self.ops)
        deps = set()
        for r in op.reads:
            w = self.last_writer.get(r)
            if w is not None:
                deps.add((w, True))
        for wkey in op.writes:
            w = self.last_writer.get(wkey)
            if w is not None:
                deps.add((w, False))
            rd = self.readers.get(wkey)
            if rd:
                for j in rd[0].values():
                    deps.add((j, False))
                for j in rd[1]:
                    deps.add((j, False))
        for j, raw in deps:
            d = self.ops[j]
            if (not d.dma) and (not dma) and d.eng == eng:
                if eng == "tensor" or not raw:
                    continue
            op.waits.add(j)
            d.inc = True
        for r in op.reads:
            rd = self.readers.setdefault(r, ({}, []))
            if dma:
                rd[1].append(op.idx)
            else:
                rd[0][eng] = op.idx
        for wkey in op.writes:
            self.last_writer[wkey] = op.idx
            self.readers[wkey] = ({}, [])
        self.ops.append(op)
        return op

    def pe(self, fn, reads=(), writes=()):
        return self.add("tensor", fn, reads, writes)

    def dve(self, fn, reads=(), writes=()):
        return self.add("vector", fn, reads, writes)

    def act(self, fn, reads=(), writes=()):
        return self.add("scalar", fn, reads, writes)

    def pool(self, fn, reads=(), writes=()):
        return self.add("gpsimd", fn, reads, writes)

    def dma(self, eng, chan, out, in_, reads=(), writes=()):
        return self.add(eng, lambda e: e.dma_start(out=out, in_=in_), reads, writes,
                        dma=True, chan=chan)

    def emit(self):
        nc = self.nc
        ops = self.ops
        last = {}
        for op in ops:
            if not op.dma:
                last[op.eng] = op
        for op in last.values():
            op.inc = True
        for op in ops:
            if op.dma:
                c = self.chan_count.get(op.chan, 0) + 16
                self.chan_count[op.chan] = c
                op.tick = c
            elif op.inc:
                self.eng_count[op.eng] += 1
                op.tick = self.eng_count[op.eng]
        chans = sorted(self.chan_count.keys(), key=str)
        engines = ["sync", "scalar", "gpsimd", "vector", "tensor"]
        per_eng = {e: [] for e in engines}
        for op in ops:
            per_eng[op.eng].append(op)
        with contextlib.ExitStack() as st:
            sems = {}
            for e in COMPUTE:
                sems[("e", e)] = st.enter_context(nc.semaphore("s_" + e))
            for i, c in enumerate(chans):
                sems[("c", c)] = st.enter_context(nc.semaphore("d%d" % i))
            block = st.enter_context(nc.Block())

            def make(ename):
                def body(eng):
                    seen = {}
                    for op in per_eng[ename]:
                        need = {}
                        for j in op.waits:
                            d = ops[j]
                            key = ("c", d.chan) if d.dma else ("e", d.eng)
                            if d.tick > need.get(key, 0):
                                need[key] = d.tick
                        for key, val in need.items():
                            if seen.get(key, 0) >= val:
                                continue
                            eng.wait_ge(sems[key], val)
                            seen[key] = val
                        ins = op.fn(eng)
                        if op.dma:
                            ins.then_inc(sems[("c", op.chan)], 16)
                        elif op.inc:
                            ins.then_inc(sems[("e", op.eng)], 1)
                    for c in chans:
                        eng.wait_ge(sems[("c", c)], self.chan_count[c])
                    for e2 in COMPUTE:
                        if self.eng_count[e2] > 0:
                            eng.wait_ge(sems[("e", e2)], self.eng_count[e2])
                return body

            for ename in engines:
                getattr(block, ename)(make(ename))


def _mk_ident(P, identf, ident):
    P.pool(lambda e: e.memset(identf[:], 0.0), writes=["identf"])
    P.pool(lambda e: e.affine_select(out=identf[:], in_=identf[:], pattern=[[-1, 128]],
                                     compare_op=ALU.not_equal, fill=1.0, base=0,
                                     channel_multiplier=1),
           reads=["identf"], writes=["identf"])
    P.dve(lambda e: e.tensor_copy(out=ident[:], in_=identf[:]), reads=["identf"], writes=["ident"])


def _rstd(P, out_ap, in_ap, inv_n, rkeys, wkey, tmp_ap, tmpkey):
    P.act(lambda e: e.activation(out=tmp_ap, in_=in_ap, func=AF.Ln, scale=inv_n, bias=EPS),
          reads=rkeys, writes=[tmpkey])
    P.act(lambda e: e.activation(out=out_ap, in_=tmp_ap, func=AF.Exp, scale=-0.5),
          reads=[tmpkey], writes=[wkey])


def phase_A(nc, T):
    with contextlib.ExitStack() as st:
        def sb(name, shape, dt):
            return st.enter_context(nc.sbuf_tensor(name, shape, dt))

        GB = 9
        uT = sb("uT", [128, 16, GB * 128], BF16)
        xt = [sb("xt%d" % i, [128, 2048], F32) for i in range(2)]
        ub = [sb("ub%d" % i, [128, 2048], BF16) for i in range(2)]
        junk = sb("junk", [128, 2048], BF16)
        wp = [sb("wp%d" % i, [128, 16, 512], BF16) for i in range(2)]
        gmix = sb("gmix", [128, 2048], F32)
        ck = sb("ck_s", [128, 33, 32], F32)
        sk = sb("sk_s", [128, 33, 32], F32)
        cq = sb("cq_s", [128, 16, 32], F32)
        sq = sb("sq_s", [128, 16, 32], F32)
        graw = sb("graw", [128, 2, 64], F32)
        gq = sb("gq", [128, 8, 64], F32)
        gk = sb("gk", [128, 8, 64], F32)
        ident = sb("ident", [128, 128], BF16)
        identf = sb("identf", [128, 128], F32)
        stat = sb("stat", [128, 16], F32)
        nst = sb("nst", [128, 2, 24], F32)
        t1 = [sb("t1_%d" % i, [128, 8, 64], F32) for i in range(2)]
        t2 = [sb("t2_%d" % i, [128, 8, 64], F32) for i in range(2)]
        rt = [sb("rt%d" % i, [128, 4, 8, 32], F32) for i in range(2)]
        yrot = [sb("yrot%d" % i, [128, 8, 64], BF16) for i in range(2)]
        kst = [sb("kst%d" % i, [128, 4, 512], BF16) for i in range(2)]
        vst = [sb("vst%d" % i, [128, 4, 4, 128], BF16) for i in range(2)]
        sst = [sb("sst%d" % i, [128, 512], BF16) for i in range(2)]
        ps = [st.enter_context(nc.psum_tensor("psA%d" % i, [128, 512], F32)) for i in range(8)]

        P = Prog(nc)
        _mk_ident(P, identf, ident)
        P.dma("sync", "gmix", gmix[:], T["gmix_b"], writes=["gmix"])
        P.dma("sync", "ck", ck[:], T["ck"], writes=["ck"])
        P.dma("sync", "sk", sk[:], T["sk"], writes=["sk"])
        P.dma("sync", "cq", cq[:], T["cq"], writes=["cq"])
        P.dma("sync", "sq", sq[:], T["sq"], writes=["sq"])
        P.dma("sync", "graw", graw[:], T["qkg_b"], writes=["graw"])
        for g in range(8):
            P.dve(lambda e, g=g: e.tensor_scalar(out=gq[:, g, :], in0=graw[:, 0, :], scalar1=0.125,
                                                 scalar2=None, op0=ALU.mult),
                  reads=["graw"], writes=["gq"])
            P.dve(lambda e, g=g: e.tensor_copy(out=gk[:, g, :], in_=graw[:, 1, :]),
                  reads=["graw"], writes=["gk"])

        w_in = T["w_in"]
        groups = [("k", list(range(0, 9))), ("k", list(range(9, 17))), ("k", list(range(17, 25))),
                  ("k", list(range(25, 33))), ("q", list(range(0, 8))), ("q", list(range(8, 16)))]
        pieces = []
        for gi, (kind, blks) in enumerate(groups):
            if kind == "k":
                for job in ("dk", "dv", "sk", "sv"):
                    for j in range(2):
                        pieces.append((gi, job, j))
            else:
                for job in ("dq", "sq"):
                    for j in range(2):
                        pieces.append((gi, job, j))
        colbase = {"dq": 0, "dk": 1024, "dv": 2048, "sq": 3072, "sk": 4096, "sv": 5120}

        def load_piece(n):
            gi, job, j = pieces[n]
            c0 = colbase[job] + 512 * j
            src = w_in[:, c0:c0 + 512].rearrange("(c p) n -> p c n", p=128)
            P.dma("gpsimd", "wp%d" % (n % 2), wp[n % 2][:], src, writes=["wp%d" % (n % 2)])

        cnt = {"x": 0, "mm": 0, "nr": 0, "st": 0, "vs": 0, "ss": 0, "ev": 0}

        def prologue(kind, blks):
            src = T["xk"] if kind == "k" else T["xq"]
            for bi, blk in enumerate(blks):
                n = cnt["x"]
                cnt["x"] += 1
                s2 = n % 2
                P.dma("sync", "xt%d" % s2, xt[s2][:], src[blk * 128:(blk + 1) * 128, :],
                      writes=["xt%d" % s2])
                sc = stat[:, 4 * s2:4 * s2 + 1]
                P.act(lambda e, s2=s2, sc=sc: e.activation(out=junk[:], in_=xt[s2][:], func=AF.Square,
                                                           accum_out=sc),
                      reads=["xt%d" % s2], writes=["junk", "st_a%d" % s2])
                _rstd(P, stat[:, 4 * s2 + 2:4 * s2 + 3], sc, 1.0 / D, ["st_a%d" % s2], "st_c%d" % s2,
                      stat[:, 4 * s2 + 1:4 * s2 + 2], "st_b%d" % s2)
                P.dve(lambda e, s2=s2: e.scalar_tensor_tensor(
                    out=ub[s2][:], in0=xt[s2][:], scalar=stat[:, 4 * s2 + 2:4 * s2 + 3], in1=gmix[:],
                    op0=ALU.mult, op1=ALU.mult),
                    reads=["xt%d" % s2, "st_c%d" % s2, "gmix"], writes=["ub%d" % s2])
                for c in range(16):
                    bank = 2 * s2 + c // 8
                    pv = ps[bank][:].bitcast(BF16)
                    P.pe(lambda e, c=c, pv=pv, s2=s2: e.transpose(
                        out=pv[:, (c % 8) * 128:(c % 8 + 1) * 128], in_=ub[s2][:, c * 128:(c + 1) * 128],
                        identity=ident[:]),
                        reads=["ub%d" % s2, "ident"], writes=["psA%d" % bank])
                for hh in range(2):
                    bank = 2 * s2 + hh
                    pv = ps[bank][:].bitcast(BF16).rearrange("p (c t) -> p c t", c=8)
                    dst = uT[:, hh * 8:(hh + 1) * 8, bi * 128:(bi + 1) * 128]
                    if hh == 0:
                        P.act(lambda e, pv=pv, dst=dst: e.copy(out=dst, in_=pv),
                              reads=["psA%d" % bank], writes=[("uT", bi, hh)])
                    else:
                        P.dve(lambda e, pv=pv, dst=dst: e.tensor_copy(out=dst, in_=pv),
                              reads=["psA%d" % bank], writes=[("uT", bi, hh)])

        def mm_bank():
            b = 4 + cnt["mm"] % 3
            cnt["mm"] += 1
            return b

        def tokmajor_mm(bi, slot, bank):
            for c in range(16):
                P.pe(lambda e, c=c, bi=bi, slot=slot, bank=bank: e.matmul(
                    ps[bank][:], lhsT=uT[:, c, bi * 128:(bi + 1) * 128], rhs=wp[slot][:, c, :],
                    start=(c == 0), stop=(c == 15)),
                    reads=[("uT", bi, 0), ("uT", bi, 1), "wp%d" % slot], writes=["psA%d" % bank])

        def job_plain(kind, blks, slot, dst, h0):
            nb = len(blks)
            for bi, blk in enumerate(blks):
                bank = mm_bank()
                tokmajor_mm(bi, slot, bank)
                sg = bi // 4
                vs = cnt["vs"] % 2
                dstv = vst[vs][:, :, bi % 4, :]
                srcv = ps[bank][:].rearrange("p (h d) -> p h d", h=4)
                ev = cnt["ev"]
                cnt["ev"] += 1
                if ev % 2 == 0:
                    P.act(lambda e, dstv=dstv, srcv=srcv: e.copy(out=dstv, in_=srcv),
                          reads=["psA%d" % bank], writes=["vst%d" % vs])
                else:
                    P.dve(lambda e, dstv=dstv, srcv=srcv: e.tensor_copy(out=dstv, in_=srcv),
                          reads=["psA%d" % bank], writes=["vst%d" % vs])
                if bi % 4 == 3 or bi == nb - 1:
                    n_in = bi % 4 + 1
                    b0 = blks[bi - (bi % 4)]
                    d_ap = dst[h0:h0 + 4, :, b0:b0 + n_in, :].rearrange("h p b d -> p h b d")
                    P.dma("sync", "vst%d" % vs, d_ap, vst[vs][:, :, 0:n_in, :],
                          reads=["vst%d" % vs], writes=[("scr", id(dst), h0, b0)])
                    cnt["vs"] += 1

        def job_normrope(kind, blks, slot, dst, h0):
            nb = len(blks)
            gain = gk if kind == "k" else gq
            ctab = ck if kind == "k" else cq
            stab = sk if kind == "k" else sq
            for bi, blk in enumerate(blks):
                bank = mm_bank()
                tokmajor_mm(bi, slot, bank)
                r = cnt["nr"] % 2
                cnt["nr"] += 1
                bk = "psA%d" % bank
                pv3 = ps[bank][:].rearrange("p (g d) -> p g d", g=8)
                P.act(lambda e, r=r, pv3=pv3: e.activation(out=t1[r][:], in_=pv3, func=AF.Square),
                      reads=[bk], writes=["t1_%d" % r])
                P.dve(lambda e, r=r: e.tensor_reduce(out=nst[:, r, 0:8], in_=t1[r][:], axis=AX.X, op=ALU.add),
                      reads=["t1_%d" % r], writes=["nsa%d" % r])
                _rstd(P, nst[:, r, 16:24], nst[:, r, 0:8], 1.0 / 64, ["nsa%d" % r], "nsc%d" % r,
                      nst[:, r, 8:16], "nsb%d" % r)
                P.dve(lambda e, r=r, pv3=pv3: e.tensor_tensor(
                    out=t2[r][:], in0=pv3, in1=nst[:, r, 16:24].unsqueeze(2).to_broadcast([128, 8, 64]),
                    op=ALU.mult),
                    reads=[bk, "nsc%d" % r], writes=["t2_%d" % r])
                P.dve(lambda e, r=r, gain=gain: e.tensor_tensor(out=t1[r][:], in0=t2[r][:], in1=gain[:],
                                                                op=ALU.mult),
                      reads=["t2_%d" % r, "gq", "gk"], writes=["t1_%d" % r])
                cb = ctab[:, blk, :].unsqueeze(1).to_broadcast([128, 8, 32])
                sbb = stab[:, blk, :].unsqueeze(1).to_broadcast([128, 8, 32])
                x1 = t1[r][:, :, 0:32]
                x2 = t1[r][:, :, 32:64]
                tabs = ["ck", "sk", "cq", "sq"]
                P.dve(lambda e, r=r, x1=x1, cb=cb: e.tensor_tensor(out=rt[r][:, 0], in0=x1, in1=cb, op=ALU.mult),
                      reads=["t1_%d" % r] + tabs, writes=[("rt", r, 0)])
                P.dve(lambda e, r=r, x2=x2, sbb=sbb: e.tensor_tensor(out=rt[r][:, 1], in0=x2, in1=sbb, op=ALU.mult),
                      reads=["t1_%d" % r] + tabs, writes=[("rt", r, 1)])
                P.dve(lambda e, r=r, x2=x2, cb=cb: e.tensor_tensor(out=rt[r][:, 2], in0=x2, in1=cb, op=ALU.mult),
                      reads=["t1_%d" % r] + tabs, writes=[("rt", r, 2)])
                P.dve(lambda e, r=r, x1=x1, sbb=sbb: e.tensor_tensor(out=rt[r][:, 3], in0=x1, in1=sbb, op=ALU.mult),
                      reads=["t1_%d" % r] + tabs, writes=[("rt", r, 3)])
                P.dve(lambda e, r=r: e.tensor_tensor(out=yrot[r][:, :, 0:32], in0=rt[r][:, 0], in1=rt[r][:, 1],
                                                     op=ALU.subtract),
                      reads=[("rt", r, 0), ("rt", r, 1)], writes=[("yrot", r, 0)])
                P.dve(lambda e, r=r: e.tensor_tensor(out=yrot[r][:, :, 32:64], in0=rt[r][:, 2], in1=rt[r][:, 3],
                                                     op=ALU.add),
                      reads=[("rt", r, 2), ("rt", r, 3)], writes=[("yrot", r, 1)])
                pv = ps[7][:].bitcast(BF16)
                yflat = yrot[r][:].rearrange("p g d -> p (g d)")
                for hh in range(4):
                    P.pe(lambda e, hh=hh, pv=pv, yflat=yflat: e.transpose(
                        out=pv[:, hh * 128:(hh + 1) * 128], in_=yflat[:, hh * 128:(hh + 1) * 128],
                        identity=ident[:]),
                        reads=[("yrot", r, 0), ("yrot", r, 1), "ident"], writes=["psA7"])
                ks = cnt["st"] % 2
                dstv = kst[ks][:, :, (bi % 4) * 128:(bi % 4 + 1) * 128]
                srcv = pv[:, 0:512].rearrange("p (h t) -> p h t", h=4)
                P.act(lambda e, dstv=dstv, srcv=srcv: e.copy(out=dstv, in_=srcv),
                      reads=["psA7"], writes=["kst%d" % ks])
                if bi % 4 == 3 or bi == nb - 1:
                    n_in = bi % 4 + 1
                    b0 = blks[bi - (bi % 4)]
                    d_ap = dst[h0:h0 + 4, :, b0 * 128:(b0 + n_in) * 128].rearrange("h p t -> p h t")
                    P.dma("sync", "kst%d" % ks, d_ap, kst[ks][:, :, 0:n_in * 128],
                          reads=["kst%d" % ks], writes=[("scr", id(dst), h0, b0)])
                    cnt["st"] += 1

        def job_transposed(kind, blks, slot, dst, h0, scale):
            ntok = len(blks) * 128
            tok0 = blks[0] * 128
            for hh in range(4):
                t0 = 0
                while t0 < ntok:
                    n = min(512, ntok - t0)
                    bank = mm_bank()
                    keys = []
                    for bb in range(t0 // 128, (t0 + n) // 128):
                        keys += [("uT", bb, 0), ("uT", bb, 1)]
                    for c in range(16):
                        P.pe(lambda e, c=c, hh=hh, t0=t0, n=n, bank=bank, slot=slot: e.matmul(
                            ps[bank][:, 0:n], lhsT=wp[slot][:, c, hh * 128:(hh + 1) * 128],
                            rhs=uT[:, c, t0:t0 + n], start=(c == 0), stop=(c == 15)),
                            reads=keys + ["wp%d" % slot], writes=["psA%d" % bank])
                    s2 = cnt["ss"] % 2
                    cnt["ss"] += 1
                    ev = cnt["ev"]
                    cnt["ev"] += 1
                    if scale != 1.0 or ev % 2 == 0:
                        P.act(lambda e, s2=s2, n=n, bank=bank: e.activation(
                            out=sst[s2][:, 0:n], in_=ps[bank][:, 0:n], func=AF.Copy, scale=scale),
                            reads=["psA%d" % bank], writes=["sst%d" % s2])
                    else:
                        P.dve(lambda e, s2=s2, n=n, bank=bank: e.tensor_copy(out=sst[s2][:, 0:n],
                                                                             in_=ps[bank][:, 0:n]),
                              reads=["psA%d" % bank], writes=["sst%d" % s2])
                    P.dma("sync", "sst%d" % s2, dst[h0 + hh, :, tok0 + t0:tok0 + t0 + n], sst[s2][:, 0:n],
                          reads=["sst%d" % s2], writes=[("scr", id(dst), h0 + hh, tok0 + t0)])
                    t0 += n

        load_piece(0)
        cur_group = -1
        for n, (gi, job, j) in enumerate(pieces):
            kind, blks = groups[gi]
            if gi != cur_group:
                prologue(kind, blks)
                cur_group = gi
            if n + 1 < len(pieces):
                load_piece(n + 1)
            slot = n % 2
            h0 = 4 * j
            if job == "dk":
                job_normrope("k", blks, slot, T["kd"], h0)
            elif job == "dq":
                job_normrope("q", blks, slot, T["qd"], h0)
            elif job == "dv":
                job_plain("k", blks, slot, T["vd"], h0)
            elif job == "sv":
                job_plain("k", blks, slot, T["vs"], h0)
            elif job == "sk":
                job_transposed("k", blks, slot, T["ks"], h0, 1.0)
            elif job == "sq":
                job_transposed("q", blks, slot, T["qs"], h0, 128.0 ** -0.5)
        P.emit()


def phase_B(nc, T, mixT, debug=False):
    with contextlib.ExitStack() as st:
        def sb(name, shape, dt):
            return st.enter_context(nc.sbuf_tensor(name, shape, dt))

        KT = [sb("KT%d" % i, [128, L], BF16) for i in range(2)]
        V = [sb("V%d" % i, [128, NB, 128], BF16) for i in range(2)]
        QT = [sb("QT%d" % i, [128, TOWN], BF16) for i in range(2)]
        maskf = sb("maskf", [128, 4, 128], F32)
        maskb = sb("maskb", [128, 4, 128], BF16)
        onesf = sb("onesf", [128, 128], F32)
        ones = sb("ones", [128, 128], BF16)
        ones_m = sb("ones_m", [128, 128], BF16)
        tri = sb("tri", [128, 128], BF16)
        tri_m = sb("tri_m", [128, 128], BF16)
        lamb = sb("lamb", [128, 4, 64], F32)
        lt = sb("lt", [128, 2, 64], F32)
        ls = sb("ls", [128, 8], F32)
        og = sb("og", [128, 2], F32)
        gcol = sb("gcol", [128, 2], F32)
        Pb = [sb("Pb%d" % i, [128, 2, 512], BF16) for i in range(3)]
        eb = [sb("eb%d" % i, [128, 512], F32) for i in range(2)]
        spb = [sb("spb%d" % i, [128, 512], BF16) for i in range(2)]
        wb = [sb("wb%d" % i, [128, 512], F32) for i in range(2)]
        Ab = [sb("Ab%d" % i, [128, 512], BF16) for i in range(2)]
        acc = [sb("acc%d" % i, [128, 512], BF16) for i in range(2)]
        R1 = sb("R1", [128, 512], F32)
        T1 = sb("T1", [128, 512], F32)
        T2 = sb("T2", [128, 512], F32)
        ob = sb("ob", [128, 512], F32)
        sqb = sb("sqb", [128, 512], BF16)
        lnv = sb("lnv", [128, 512], F32)
        rsb = sb("rsb", [128, 512], F32)
        Dp = [st.enter_context(nc.psum_tensor("psB%d" % i, [128, 1024], F32)) for i in range(4)]

        P = Prog(nc)
        P.dma("sync", "maskf", maskf[:], T["masks"], writes=["maskf"])
        P.dve(lambda e: e.tensor_copy(out=maskb[:], in_=maskf[:]), reads=["maskf"], writes=["maskb"])
        P.pool(lambda e: e.memset(onesf[:], 1.0), writes=["onesf"])
        P.dve(lambda e: e.tensor_copy(out=ones[:], in_=onesf[:]), reads=["onesf"], writes=["ones"])
        P.pool(lambda e: e.affine_select(out=onesf[:], in_=onesf[:], pattern=[[0, 128]], compare_op=ALU.is_ge,
                                         fill=0.0, base=-PAD, channel_multiplier=1),
               reads=["ones", "onesf"], writes=["onesf"])
        P.dve(lambda e: e.tensor_copy(out=ones_m[:], in_=onesf[:]), reads=["onesf"], writes=["ones_m"])
        P.pool(lambda e: e.memset(onesf[:], 1.0), reads=["ones_m"], writes=["onesf"])
        P.pool(lambda e: e.affine_select(out=onesf[:], in_=onesf[:], pattern=[[-1, 128]], compare_op=ALU.is_ge,
                                         fill=0.0, base=0, channel_multiplier=1),
               reads=["onesf"], writes=["onesf"])
        P.dve(lambda e: e.tensor_copy(out=tri[:], in_=onesf[:]), reads=["onesf"], writes=["tri"])
        P.pool(lambda e: e.affine_select(out=onesf[:], in_=onesf[:], pattern=[[0, 128]], compare_op=ALU.is_ge,
                                         fill=0.0, base=-PAD, channel_multiplier=1),
               reads=["tri", "onesf"], writes=["onesf"])
        P.dve(lambda e: e.tensor_copy(out=tri_m[:], in_=onesf[:]), reads=["onesf"], writes=["tri_m"])
        P.dma("sync", "lamb", lamb[:], T["lam_b"], writes=["lamb"])
        P.dve(lambda e: e.tensor_tensor(out=lt[:, 0, :], in0=lamb[:, 0, :], in1=lamb[:, 1, :], op=ALU.mult),
              reads=["lamb"], writes=["lt0"])
        P.dve(lambda e: e.tensor_tensor(out=lt[:, 1, :], in0=lamb[:, 2, :], in1=lamb[:, 3, :], op=ALU.mult),
              reads=["lamb"], writes=["lt1"])
        P.dve(lambda e: e.tensor_reduce(out=ls[:, 0:2], in_=lt[:], axis=AX.X, op=ALU.add),
              reads=["lt0", "lt1"], writes=["ls_a"])
        P.act(lambda e: e.activation(out=ls[:, 2:4], in_=ls[:, 0:2], func=AF.Exp), reads=["ls_a"], writes=["ls_b"])
        P.dve(lambda e: e.tensor_tensor(out=ls[:, 4:5], in0=ls[:, 3:4], in1=ls[:, 2:3], op=ALU.subtract),
              reads=["ls_b"], writes=["ls_c"])
        P.dve(lambda e: e.tensor_scalar(out=ls[:, 5:6], in0=ls[:, 4:5], scalar1=-LAMBDA_INIT, scalar2=None,
                                        op0=ALU.add),
              reads=["ls_c"], writes=["neg_lam"])
        neg_lam = ls[:, 5:6]
        P.dma("sync", "og", og[:], T["og_col"], writes=["og"])
        P.dve(lambda e: e.tensor_scalar(out=gcol[:, 0:1], in0=og[:, 0:1], scalar1=1.0 - LAMBDA_INIT, scalar2=None,
                                        op0=ALU.mult),
              reads=["og"], writes=["gcol0"])
        P.dve(lambda e: e.tensor_copy(out=gcol[:, 1:2], in_=og[:, 1:2]), reads=["og"], writes=["gcol1"])

        def load_head(hd):
            s = hd % 2
            if hd < 8:
                ksrc, vsrc, qsrc, h = T["kd"], T["vd"], T["qd"], hd
            else:
                ksrc, vsrc, qsrc, h = T["ks"], T["vs"], T["qs"], hd - 8
            P.dma("sync", "KT%d" % s, KT[s][:], ksrc[h], writes=["KT%d" % s])
            P.dma("sync", "V%d" % s, V[s][:], vsrc[h], writes=["V%d" % s])
            P.dma("sync", "QT%d" % s, QT[s][:], qsrc[h], writes=["QT%d" % s])

        def step_geom(i, kb):
            jmin = max(0, (kb - 1) // 2 - 4 * i) if kb >= 1 else 0
            c0 = 128 * jmin
            masked = kb >= 8 * i + 1
            mi = 0 if (kb % 2 == 1) else 1
            return c0, masked, mi

        cntp = {"p": 0}

        def head_norm(hd, i, src_ap, src_keys, ss_ap, ss_key, gi):
            q0 = 512 * i
            P.act(lambda e: e.activation(out=sqb[:], in_=src_ap, func=AF.Square), reads=src_keys, writes=["sqb"])
            P.pe(lambda e: e.matmul(ss_ap, lhsT=ones[:], rhs=sqb[:], start=True, stop=True),
                 reads=["sqb", "ones"], writes=[ss_key])
            _rstd(P, rsb[:], ss_ap, 1.0 / 128, [ss_key], "rsb", lnv[:], "lnv")
            P.dve(lambda e: e.scalar_tensor_tensor(out=mixT[:, hd, q0:q0 + 512], in0=src_ap,
                                                   scalar=gcol[:, gi:gi + 1], in1=rsb[:],
                                                   op0=ALU.mult, op1=ALU.mult),
                  reads=src_keys + ["rsb", "gcol%d" % gi], writes=[("mixT", hd, i)])

        def diff_head(hd):
            s = hd % 2
            kt, v, qt = KT[s], V[s], QT[s]
            hk = ["KT%d" % s, "QT%d" % s]
            O1, L1 = Dp[2][:, 0:512], Dp[2][:, 512:1024]
            O2, L2 = Dp[3][:, 0:512], Dp[3][:, 512:1024]

            def do_tile(i):
                q0 = 512 * i
                nk = 8 * i + 9
                steps = list(range(nk))

                def qk(kb):
                    c0, _, _ = step_geom(i, kb)
                    d = kb % 2
                    for c in range(2):
                        P.pe(lambda e, c=c, kb=kb, c0=c0, d=d: e.matmul(
                            Dp[d][:, c * 512 + c0:c * 512 + 512],
                            lhsT=kt[c * 64:(c + 1) * 64, kb * 128:(kb + 1) * 128],
                            rhs=qt[c * 64:(c + 1) * 64, q0 + c0:q0 + 512], start=True, stop=True),
                            reads=hk, writes=[("S", d, c)])

                qk(0)
                for kb in steps:
                    if kb + 1 < nk:
                        qk(kb + 1)
                    c0, masked, mi = step_geom(i, kb)
                    d = kb % 2
                    pslot = cntp["p"] % 3
                    cntp["p"] += 1
                    pb = Pb[pslot]
                    pk = "Pb%d" % pslot
                    sv = Dp[d][:].rearrange("p (c n) -> p c n", c=2)[:, :, c0:512]
                    P.act(lambda e, pb=pb, sv=sv, c0=c0: e.activation(out=pb[:, :, c0:512], in_=sv, func=AF.Exp),
                          reads=[("S", d, 0), ("S", d, 1)], writes=[pk])
                    if masked:
                        mk = maskb[:, mi, :].unsqueeze(1).to_broadcast([128, 2, 128])
                        P.dve(lambda e, pb=pb, c0=c0, mk=mk: e.tensor_tensor(
                            out=pb[:, :, c0:c0 + 128], in0=pb[:, :, c0:c0 + 128], in1=mk, op=ALU.mult),
                            reads=[pk, "maskb"], writes=[pk])
                    first = (kb == 0)
                    lastk = (kb == nk - 1)
                    onem = ones_m if kb == 0 else ones
                    for c, (Oc, Lc, ok, lk) in enumerate(((O1, L1, "O1", "L1"), (O2, L2, "O2", "L2"))):
                        P.pe(lambda e, c=c, Oc=Oc, pb=pb, c0=c0, kb=kb, first=first, lastk=lastk: e.matmul(
                            Oc[:, c0:512], lhsT=v[:, kb, :], rhs=pb[:, c, c0:512], start=first, stop=lastk),
                            reads=[pk, "V%d" % s], writes=[ok])
                        P.pe(lambda e, c=c, Lc=Lc, pb=pb, c0=c0, onem=onem, first=first, lastk=lastk: e.matmul(
                            Lc[:, c0:512], lhsT=onem[:], rhs=pb[:, c, c0:512], start=first, stop=lastk),
                            reads=[pk, "ones", "ones_m"], writes=[lk])
                P.dve(lambda e: e.reciprocal(out=R1[:], in_=L1), reads=["L1"], writes=["R1"])
                P.dve(lambda e: e.tensor_tensor(out=T1[:], in0=O1, in1=R1[:], op=ALU.mult),
                      reads=["O1", "R1"], writes=["T1"])
                P.dve(lambda e: e.reciprocal(out=R1[:], in_=L2), reads=["L2", "T1"], writes=["R1"])
                P.dve(lambda e: e.tensor_tensor(out=T2[:], in0=O2, in1=R1[:], op=ALU.mult),
                      reads=["O2", "R1"], writes=["T2"])
                P.dve(lambda e: e.scalar_tensor_tensor(out=ob[:], in0=T2[:], scalar=neg_lam, in1=T1[:],
                                                       op0=ALU.mult, op1=ALU.add),
                      reads=["T1", "T2", "neg_lam"], writes=["ob"])
                head_norm(hd, i, ob[:], ["ob"], L1, "L1", 0)

            for i in range(4):
                do_tile(i)

        def sb_head(hd):
            s = hd % 2
            kt, v, qt = KT[s], V[s], QT[s]
            hk = ["KT%d" % s, "QT%d" % s]

            def do_tile(i):
                q0 = 512 * i
                nk = 8 * i + 9
                order = list(range(nk - 1, -1, -1))
                a = acc[i % 2]
                ak = "acc%d" % (i % 2)
                Ob = Dp[2 + i % 2][:, 0:512]
                Ok = "SO%d" % (i % 2)
                SSb = Dp[2 + i % 2][:, 512:1024]
                SSk = "SS%d" % (i % 2)
                P.pool(lambda e, a=a: e.memset(a[:], 0.0), writes=[ak])

                def zmm(t):
                    kb = order[t]
                    c0, _, _ = step_geom(i, kb)
                    d = t % 2
                    P.pe(lambda e, kb=kb, c0=c0, d=d: e.matmul(
                        Dp[0][:, d * 512 + c0:d * 512 + 512], lhsT=kt[:, kb * 128:(kb + 1) * 128],
                        rhs=qt[:, q0 + c0:q0 + 512], start=True, stop=True),
                        reads=hk, writes=[("Z", d)])

                def stage_a(t):
                    kb = order[t]
                    c0, masked, mi = step_geom(i, kb)
                    d = t % 2
                    P.act(lambda e, d=d, c0=c0: e.activation(out=eb[d][:, c0:512],
                                                             in_=Dp[0][:, d * 512 + c0:d * 512 + 512], func=AF.Exp),
                          reads=[("Z", d)], writes=["eb%d" % d])
                    P.act(lambda e, d=d, c0=c0: e.activation(out=spb[d][:, c0:512], in_=eb[d][:, c0:512],
                                                             func=AF.Ln, bias=1.0),
                          reads=["eb%d" % d], writes=["spb%d" % d])
                    if masked:
                        P.dve(lambda e, d=d, c0=c0, mi=mi: e.tensor_tensor(
                            out=spb[d][:, c0:c0 + 128], in0=spb[d][:, c0:c0 + 128], in1=maskb[:, 2 + mi, :],
                            op=ALU.mult),
                            reads=["spb%d" % d, "maskb"], writes=["spb%d" % d])
                    tr = tri_m if kb == 0 else tri
                    only = (t == 0)
                    P.pe(lambda e, d=d, c0=c0, tr=tr, only=only: e.matmul(
                        Dp[1][:, d * 512 + c0:d * 512 + 512], lhsT=tr[:], rhs=spb[d][:, c0:512],
                        start=True, stop=only),
                        reads=["spb%d" % d, "tri", "tri_m"], writes=[("C", d)])
                    if not only:
                        P.pe(lambda e, d=d, c0=c0, a=a: e.matmul(
                            Dp[1][:, d * 512 + c0:d * 512 + 512], lhsT=ones[:], rhs=a[:, c0:512],
                            start=False, stop=True),
                            reads=[ak, "ones"], writes=[("C", d)])
                    if t != nk - 1:
                        P.pool(lambda e, d=d, c0=c0, a=a: e.tensor_tensor(
                            out=a[:, c0:512], in0=a[:, c0:512], in1=spb[d][:, c0:512], op=ALU.add),
                            reads=[ak, "spb%d" % d], writes=[ak])

                def stage_b(t):
                    kb = order[t]
                    c0, masked, mi = step_geom(i, kb)
                    d = t % 2
                    P.act(lambda e, d=d, c0=c0: e.activation(out=wb[d][:, c0:512],
                                                             in_=Dp[1][:, d * 512 + c0:d * 512 + 512],
                                                             func=AF.Exp, scale=-1.0),
                          reads=[("C", d)], writes=["wb%d" % d])
                    P.dve(lambda e, d=d, c0=c0: e.tensor_tensor(out=Ab[d][:, c0:512], in0=eb[d][:, c0:512],
                                                                in1=wb[d][:, c0:512], op=ALU.mult),
                          reads=["eb%d" % d, "wb%d" % d], writes=["Ab%d" % d])
                    if masked:
                        P.dve(lambda e, d=d, c0=c0, mi=mi: e.tensor_tensor(
                            out=Ab[d][:, c0:c0 + 128], in0=Ab[d][:, c0:c0 + 128], in1=maskb[:, 2 + mi, :],
                            op=ALU.mult),
                            reads=["Ab%d" % d, "maskb"], writes=["Ab%d" % d])
                    P.pe(lambda e, d=d, c0=c0, kb=kb, t=t: e.matmul(
                        Ob[:, c0:512], lhsT=v[:, kb, :], rhs=Ab[d][:, c0:512],
                        start=(t == 0), stop=(t == nk - 1), skip_group_check=True),
                        reads=["Ab%d" % d, "V%d" % s], writes=[Ok])

                zmm(0)
                for t in range(nk):
                    if t + 1 < nk:
                        zmm(t + 1)
                    stage_a(t)
                    if t >= 1:
                        stage_b(t - 1)
                stage_b(nk - 1)
                head_norm(hd, i, Ob, [Ok], SSb, SSk, 1)

            for i in range(4):
                do_tile(i)

        load_head(0)
        for hd in range(16):
            if hd + 1 < 16:
                load_head(hd + 1)
            if hd < 8:
                diff_head(hd)
            else:
                sb_head(hd)
        if debug:
            P.dma("sync", "mixdump", T["mixdump"], mixT[:],
                  reads=[("mixT", hd, i) for hd in range(16) for i in range(4)], writes=["mixdump"])
        P.emit()


def phase_C(nc, T, mixT, rstd2, debug=False, hasB=True):
    with contextlib.ExitStack() as st:
        def sb(name, shape, dt):
            return st.enter_context(nc.sbuf_tensor(name, shape, dt))

        wo = [sb("wo%d" % i, [128, 16, 512], BF16) for i in range(2)]
        xr = [sb("xr%d" % i, [128, 512], F32) for i in range(3)]
        h1t = [sb("h1t%d" % i, [128, 512], F32) for i in range(3)]
        junk = sb("junkC", [128, 512], BF16)
        ssq = sb("ssq", [128, 16, 4], F32)
        ssum = sb("ssum", [128, 16], F32)
        lnt = sb("lnt", [128, 16], F32)
        ps = [st.enter_context(nc.psum_tensor("psC%d" % i, [128, 512], F32)) for i in range(4)]
        P = Prog(nc)
        w_out = T["w_out"]

        def load_w(cg):
            src = w_out[:, cg * 512:(cg + 1) * 512].rearrange("(c p) n -> p c n", p=128)
            P.dma("gpsimd", "wo%d" % (cg % 2), wo[cg % 2][:], src, writes=["wo%d" % (cg % 2)])

        mixk = []
        if debug and not hasB:
            P.dma("sync", "mixin", mixT[:], T["mixdump"], writes=["mixin"])
            mixk = ["mixin"]
        load_w(0)
        n = 0
        for cg in range(4):
            if cg + 1 < 4:
                load_w(cg + 1)
            for tb in range(16):
                r = n % 3
                bank = n % 4
                n += 1
                P.dma("sync", "xr%d" % r, xr[r][:], T["xq"][tb * 128:(tb + 1) * 128, cg * 512:(cg + 1) * 512],
                      writes=["xr%d" % r])
                for c in range(16):
                    P.pe(lambda e, c=c, tb=tb, bank=bank, cg=cg: e.matmul(
                        ps[bank][:], lhsT=mixT[:, c, tb * 128:(tb + 1) * 128], rhs=wo[cg % 2][:, c, :],
                        start=(c == 0), stop=(c == 15)),
                        reads=["wo%d" % (cg % 2)] + mixk, writes=["psC%d" % bank])
                P.dve(lambda e, r=r, bank=bank: e.tensor_tensor(out=h1t[r][:], in0=ps[bank][:], in1=xr[r][:],
                                                                op=ALU.add),
                      reads=["psC%d" % bank, "xr%d" % r], writes=["h1t%d" % r])
                P.act(lambda e, r=r, tb=tb, cg=cg: e.activation(out=junk[:], in_=h1t[r][:], func=AF.Square,
                                                                accum_out=ssq[:, tb, cg:cg + 1]),
                      reads=["h1t%d" % r], writes=["junkC", ("ssq", tb, cg)])
                P.dma("sync", "h1t%d" % r, T["h1"][tb * 128:(tb + 1) * 128, cg * 512:(cg + 1) * 512], h1t[r][:],
                      reads=["h1t%d" % r], writes=[("h1", tb, cg)])
        allk = [("ssq", tb, cg) for tb in range(16) for cg in range(4)]
        P.dve(lambda e: e.tensor_reduce(out=ssum[:], in_=ssq[:], axis=AX.X, op=ALU.add), reads=allk, writes=["ssum"])
        _rstd(P, rstd2[:], ssum[:], 1.0 / D, ["ssum"], "rstd2", lnt[:], "lnt")
        if debug:
            P.dma("sync", "rstd2dump", T["rstd2dump"], rstd2[:], reads=["rstd2"], writes=["rstd2dump"])
        P.emit()


def phase_D(nc, T, rstd2, debug=False, hasC=True):
    with contextlib.ExitStack() as st:
        def sb(name, shape, dt):
            return st.enter_context(nc.sbuf_tensor(name, shape, dt))

        h1t = [sb("h1d%d" % i, [128, 2048], F32) for i in range(2)]
        hr = [sb("hr%d" % i, [128, 512], F32) for i in range(3)]
        gm = sb("gm", [128, 2048], F32)
        mb = [sb("mb%d" % i, [128, 2048], BF16) for i in range(2)]
        mT = sb("mT", [128, 16, 512], BF16)
        hidT = sb("hidT", [128, 64, 512], BF16)
        wu = [sb("wu%d" % i, [128, 16, 256], BF16) for i in range(3)]
        wd = [sb("wd%d" % i, [128, 8, 512], BF16) for i in range(3)]
        rl = [sb("rl%d" % i, [128, 512], BF16) for i in range(2)]
        ot = [sb("ot%d" % i, [128, 512], F32) for i in range(3)]
        ident = sb("identD", [128, 128], BF16)
        identf = sb("identfD", [128, 128], F32)
        ps = [st.enter_context(nc.psum_tensor("psD%d" % i, [128, 512], F32)) for i in range(8)]
        P = Prog(nc)
        _mk_ident(P, identf, ident)
        P.dma("sync", "gm", gm[:], T["gmlp_b"], writes=["gm"])
        rk = []
        if debug and not hasC:
            P.dma("sync", "rstd2in", rstd2[:], T["rstd2dump"], writes=["rstd2in"])
            rk = ["rstd2in"]
        w_up, w_down = T["w_up"], T["w_down"]

        sched = []
        for tt in range(4):
            for fp in range(32):
                sched.append(("u", tt, fp))
            for cg in range(4):
                for dp in range(8):
                    sched.append(("d", tt, cg, dp))
        ucnt = {"u": 0, "d": 0}
        slot_of = {}
        for it in sched:
            k = it[0]
            slot_of[it] = ucnt[k] % 3
            ucnt[k] += 1

        def issue(idx):
            it = sched[idx]
            s = slot_of[it]
            if it[0] == "u":
                fp = it[2]
                src = w_up[:, fp * 256:(fp + 1) * 256].rearrange("(c p) n -> p c n", p=128)
                P.dma("gpsimd", "wu%d" % s, wu[s][:], src, writes=["wu%d" % s])
            else:
                cg, dp = it[2], it[3]
                src = w_down[dp * 1024:(dp + 1) * 1024, cg * 512:(cg + 1) * 512].rearrange("(c p) n -> p c n", p=128)
                P.dma("gpsimd", "wd%d" % s, wd[s][:], src, writes=["wd%d" % s])

        AHEAD = 2
        nxt = {"i": 0}

        def ensure(idx):
            while nxt["i"] <= min(idx + AHEAD, len(sched) - 1):
                issue(nxt["i"])
                nxt["i"] += 1

        on = 0
        pos = 0
        for tt in range(4):
            for b in range(4):
                blk = tt * 4 + b
                m2 = b % 2
                P.dma("sync", "h1d%d" % m2, h1t[m2][:], T["h1"][blk * 128:(blk + 1) * 128, :],
                      writes=["h1d%d" % m2])
                P.dve(lambda e, b=b, blk=blk, m2=m2: e.scalar_tensor_tensor(
                    out=mb[m2][:], in0=h1t[m2][:], scalar=rstd2[:, blk:blk + 1], in1=gm[:],
                    op0=ALU.mult, op1=ALU.mult),
                    reads=["h1d%d" % m2, "gm"] + rk, writes=["mb%d" % m2])
                for c in range(16):
                    bank = 6 + c // 8
                    pv = ps[bank][:].bitcast(BF16)
                    P.pe(lambda e, c=c, pv=pv, m2=m2: e.transpose(
                        out=pv[:, (c % 8) * 128:(c % 8 + 1) * 128], in_=mb[m2][:, c * 128:(c + 1) * 128],
                        identity=ident[:]),
                        reads=["mb%d" % m2, "ident"], writes=["psD%d" % bank])
                for hh in range(2):
                    bank = 6 + hh
                    pv = ps[bank][:].bitcast(BF16).rearrange("p (c t) -> p c t", c=8)
                    dst = mT[:, hh * 8:(hh + 1) * 8, b * 128:(b + 1) * 128]
                    if hh == 0:
                        P.act(lambda e, pv=pv, dst=dst: e.copy(out=dst, in_=pv),
                              reads=["psD%d" % bank], writes=[("mT", b, hh)])
                    else:
                        P.dve(lambda e, pv=pv, dst=dst: e.tensor_copy(out=dst, in_=pv),
                              reads=["psD%d" % bank], writes=[("mT", b, hh)])
            mTk = [("mT", b, hh) for b in range(4) for hh in range(2)]
            for fp in range(32):
                ensure(pos)
                s = slot_of[sched[pos]]
                pos += 1
                for half in range(2):
                    fc = fp * 2 + half
                    bank = 4 + fc % 2
                    for c in range(16):
                        P.pe(lambda e, c=c, s=s, half=half, bank=bank: e.matmul(
                            ps[bank][:], lhsT=wu[s][:, c, half * 128:(half + 1) * 128], rhs=mT[:, c, :],
                            start=(c == 0), stop=(c == 15)),
                            reads=mTk + ["wu%d" % s], writes=["psD%d" % bank])
                    r = fc % 2
                    P.act(lambda e, r=r, bank=bank: e.activation(out=rl[r][:], in_=ps[bank][:], func=AF.Relu),
                          reads=["psD%d" % bank], writes=["rl%d" % r])
                    P.dve(lambda e, r=r, fc=fc: e.tensor_tensor(out=hidT[:, fc, :], in0=rl[r][:], in1=rl[r][:],
                                                                op=ALU.mult),
                          reads=["rl%d" % r], writes=[("hid", fc)])
            for cg in range(4):
                base = 0 if cg % 2 == 0 else 4
                for dp in range(8):
                    ensure(pos)
                    s = slot_of[sched[pos]]
                    pos += 1
                    for i8 in range(8):
                        fc = dp * 8 + i8
                        for tb in range(4):
                            P.pe(lambda e, fc=fc, i8=i8, tb=tb, s=s, base=base: e.matmul(
                                ps[base + tb][:], lhsT=hidT[:, fc, tb * 128:(tb + 1) * 128], rhs=wd[s][:, i8, :],
                                start=(fc == 0), stop=(fc == 63)),
                                reads=[("hid", fc), "wd%d" % s], writes=["psD%d" % (base + tb)])
                for tb in range(4):
                    o3 = on % 3
                    on += 1
                    blk = tt * 4 + tb
                    P.dma("sync", "hr%d" % o3, hr[o3][:], T["h1"][blk * 128:(blk + 1) * 128, cg * 512:(cg + 1) * 512],
                          writes=["hr%d" % o3])
                    P.dve(lambda e, o3=o3, tb=tb, cg=cg, base=base: e.tensor_tensor(
                        out=ot[o3][:], in0=ps[base + tb][:], in1=hr[o3][:], op=ALU.add),
                        reads=["psD%d" % (base + tb), "hr%d" % o3], writes=["ot%d" % o3])
                    P.dma("sync", "ot%d" % o3, T["y"][blk * 128:(blk + 1) * 128, cg * 512:(cg + 1) * 512], ot[o3][:],
                          reads=["ot%d" % o3], writes=[("y", blk, cg)])
        P.emit()


def build_nc(debug=False, phases="ABCD"):
    nc = bass.Bass("TRN2", target_bir_lowering=False)
    T = {}

    def din(name, shape, dt=F32):
        T[name] = nc.dram_tensor(name, shape, dt, kind="ExternalInput").ap()

    hasA, hasB, hasC, hasD = ("A" in phases), ("B" in phases), ("C" in phases), ("D" in phases)
    if hasA:
        din("xk", [L, D])
        din("w_in", [D, 6144])
        din("gmix_b", [128, D])
        din("qkg_b", [128, 2, 64])
        din("ck", [128, 33, 32])
        din("sk", [128, 33, 32])
        din("cq", [128, 16, 32])
        din("sq", [128, 16, 32])
    if hasA or hasC:
        din("xq", [TOWN, D])
    if hasC:
        din("w_out", [D, D])
    if hasD:
        din("w_up", [D, DFF])
        din("w_down", [DFF, D])
        din("gmlp_b", [128, D])
    if hasB:
        din("lam_b", [128, 4, 64])
        din("og_col", [128, 2])
        din("masks", [128, 4, 128])
    T["y"] = nc.dram_tensor("y", [TOWN, D], F32, kind="ExternalOutput").ap()
    skind = ("ExternalOutput" if debug else "Internal") if hasA else "ExternalInput"
    if hasA or hasB:
        for nm in ("kd", "ks"):
            T[nm] = nc.dram_tensor(nm, [8, 128, L], BF16, kind=skind).ap()
        for nm in ("vd", "vs"):
            T[nm] = nc.dram_tensor(nm, [8, 128, NB, 128], BF16, kind=skind).ap()
        for nm in ("qd", "qs"):
            T[nm] = nc.dram_tensor(nm, [8, 128, TOWN], BF16, kind=skind).ap()
    hkind = ("ExternalOutput" if debug else "Internal") if hasC else "ExternalInput"
    if hasC or hasD:
        T["h1"] = nc.dram_tensor("h1", [TOWN, D], F32, kind=hkind).ap()
    if debug and (hasB or hasC):
        mk = "ExternalOutput" if hasB else "ExternalInput"
        T["mixdump"] = nc.dram_tensor("mixdump", [128, 16, TOWN], BF16, kind=mk).ap()
    if debug and (hasC or hasD):
        rk = "ExternalOutput" if hasC else "ExternalInput"
        T["rstd2dump"] = nc.dram_tensor("rstd2dump", [128, 16], F32, kind=rk).ap()

    if hasA:
        phase_A(nc, T)
    with nc.sbuf_tensor("rstd2", [128, 16], F32) as rstd2:
        with nc.sbuf_tensor("mixT", [128, 16, TOWN], BF16) as mixT:
            if hasB:
                phase_B(nc, T, mixT, debug)
            if hasC:
                phase_C(nc, T, mixT, rstd2, debug, hasB)
        if hasD:
            phase_D(nc, T, rstd2, debug, hasC)
    return nc


def _rope_tables():
    pos = np.arange(L, dtype=np.float32) - np.float32(PAD)
    inv = (np.float32(10000.0) ** (-np.arange(0, 64, 2, dtype=np.float32) / np.float32(64))).astype(np.float32)
    ang = (pos[:, None] * inv[None, :]).astype(np.float32)
    return np.cos(ang).astype(np.float32), np.sin(ang).astype(np.float32)


def _prep_inputs(x, meta_tokens, g_mix, w_in, q_norm_g, k_norm_g, lambda_q1, lambda_k1, lambda_q2,
                 lambda_k2, diff_out_g, sb_out_g, w_out, g_mlp, w_up, w_down):
    f = np.float32
    x = np.asarray(x, f)
    B = x.shape[0]
    cos, sin = _rope_tables()
    ckt = np.ascontiguousarray(cos.reshape(NB, 128, 32).transpose(1, 0, 2))
    skt = np.ascontiguousarray(sin.reshape(NB, 128, 32).transpose(1, 0, 2))
    rep = lambda v, n=128: np.ascontiguousarray(np.broadcast_to(np.asarray(v, f).reshape(1, -1), (n, np.asarray(v).size)))
    gmix_b = rep(g_mix[0])
    gmlp_b = rep(g_mlp[0])
    qkg_b = np.ascontiguousarray(np.stack([rep(q_norm_g[0]), rep(k_norm_g[0])], axis=1))
    lam_b = np.ascontiguousarray(np.stack([rep(lambda_q1[0]), rep(lambda_k1[0]), rep(lambda_q2[0]),
                                           rep(lambda_k2[0])], axis=1))
    og_col = np.ascontiguousarray(np.stack([np.asarray(diff_out_g[0], f), np.asarray(sb_out_g[0], f)], axis=1))
    kk = np.arange(128)[:, None]
    qq = np.arange(128)[None, :]
    tri_le = (kk <= qq).astype(f)
    tri_lt = (kk < qq).astype(f)
    onesm = np.ones((128, 128), f)
    zerosm = np.zeros((128, 128), f)
    shared = {
        "w_in": np.ascontiguousarray(np.asarray(w_in[0], f)),
        "w_out": np.ascontiguousarray(np.asarray(w_out[0], f)),
        "w_up": np.ascontiguousarray(np.asarray(w_up[0], f)),
        "w_down": np.ascontiguousarray(np.asarray(w_down[0], f)),
        "gmix_b": gmix_b, "gmlp_b": gmlp_b, "qkg_b": qkg_b, "lam_b": lam_b, "og_col": og_col,
        "ck": ckt, "sk": skt,
    }
    in_maps = []
    own = []
    meta = np.asarray(meta_tokens, f)
    for c in range(8):
        b, p = c // 2, c % 2
        xk = np.concatenate([np.zeros((PAD, D), f), meta, x[b]], axis=0)
        blks = [2 * j + 1 + p for j in range(NOB)]
        own.append(blks)
        xq = np.ascontiguousarray(xk.reshape(NB, 128, D)[blks].reshape(TOWN, D))
        cqt = np.ascontiguousarray(ckt[:, blks, :])
        sqt = np.ascontiguousarray(skt[:, blks, :])
        if p == 0:
            masks = np.stack([tri_le, zerosm, tri_lt, zerosm], axis=1)
        else:
            masks = np.stack([onesm, tri_le, onesm, tri_lt], axis=1)
        m = dict(shared)
        m.update({"xk": np.ascontiguousarray(xk), "xq": xq, "cq": cqt, "sq": sqt,
                  "masks": np.ascontiguousarray(masks.astype(f))})
        in_maps.append(m)
    return in_maps, own, B


_NC_CACHE = {}


def kernel(x, meta_tokens, g_mix, w_in, q_norm_g, k_norm_g, lambda_q1, lambda_k1, lambda_q2, lambda_k2,
           diff_out_g, sb_out_g, w_out, g_mlp, w_up, w_down):
    in_maps, own, B = _prep_inputs(x, meta_tokens, g_mix, w_in, q_norm_g, k_norm_g, lambda_q1, lambda_k1,
                                   lambda_q2, lambda_k2, diff_out_g, sb_out_g, w_out, g_mlp, w_up, w_down)
    if "nc" not in _NC_CACHE:
        _NC_CACHE["nc"] = build_nc()
    nc = _NC_CACHE["nc"]
    res = run_bass_kernel_spmd(nc, in_maps, core_ids=list(range(8)))
    out = np.zeros((B, 4096, D), np.float32)
    for c in range(8):
        b = c // 2
        y = np.asarray(res.results[c]["y"], np.float32).reshape(NOB, 128, D)
        for j, blk in enumerate(own[c]):
            out[b, (blk - 1) * 128:blk * 128, :] = y[j]
    return out
```

```python
import contextlib
import numpy as np
import concourse.bass as bass
import concourse.mybir as mybir
from concourse.bass_utils import run_bass_kernel_spmd

F32 = mybir.dt.float32
BF16 = mybir.dt.bfloat16
AF = mybir.ActivationFunctionType
ALU = mybir.AluOpType
AX = mybir.AxisListType

D = 2048
L = 4224
NB = 33
TOWN = 2048
NOB = 16
DFF = 8192
EPS = 1e-6
PAD = 112
LAMBDA_INIT = 0.8 - 0.6 * 1.0

COMPUTE = ("tensor", "vector", "scalar", "gpsimd")


class _Op:
    __slots__ = ("eng", "fn", "reads", "writes", "dma", "chan", "waits", "inc", "tick", "idx")

    def __init__(self, eng, fn, reads, writes, dma, chan):
        self.eng = eng
        self.fn = fn
        self.reads = reads
        self.writes = writes
        self.dma = dma
        self.chan = chan
        self.waits = set()
        self.inc = False
        self.tick = None


class Prog:
    def __init__(self, nc):
        self.nc = nc
        self.ops = []
        self.last_writer = {}
        self.readers = {}
        self.eng_count = {e: 0 for e in COMPUTE}
        self.chan_count = {}

    def add(self, eng, fn, reads=(), writes=(), dma=False, chan=None):
        op = _Op(eng, fn, tuple(reads), tuple(writes), dma, chan)
        op.idx = len(self.ops)
        deps = set()
        for r in op.reads:
            w = self.last_writer.get(r)
            if w is not None:
                deps.add((w, True))
        for wkey in op.writes:
            w = self.last_writer.get(wkey)
            if w is not None:
                deps.add((w, False))
            rd = self.readers.get(wkey)
            if rd:
                for j in rd[0].values():
                    deps.add((j, False))
                for j in rd[1]:
                    deps.add((j, False))
        for j, raw in deps:
            d = self.ops[j]
            if (not d.dma) and (not dma) and d.eng == eng:
                if eng == "tensor":
                    continue
            op.waits.add(j)
            d.inc = True
        for r in op.reads:
            rd = self.readers.setdefault(r, ({}, []))
            if dma:
                rd[1].append(op.idx)
            else:
                rd[0][eng] = op.idx
        for wkey in op.writes:
            self.last_writer[wkey] = op.idx
            self.readers[wkey] = ({}, [])
        self.ops.append(op)
        return op

    def pe(self, fn, reads=(), writes=()):
        return self.add("tensor", fn, reads, writes)

    def dve(self, fn, reads=(), writes=()):
        return self.add("vector", fn, reads, writes)

    def act(self, fn, reads=(), writes=()):
        return self.add("scalar", fn, reads, writes)

    def pool(self, fn, reads=(), writes=()):
        return self.add("gpsimd", fn, reads, writes)

    def dma(self, eng, chan, out, in_, reads=(), writes=()):
        return self.add(eng, lambda e: e.dma_start(out=out, in_=in_), reads, writes,
                        dma=True, chan=chan)

    def emit(self):
        nc = self.nc
        ops = self.ops
        last = {}
        for op in ops:
            if not op.dma:
                last[op.eng] = op
        for op in last.values():
            op.inc = True
        for op in ops:
            if op.dma:
                c = self.chan_count.get(op.chan, 0) + 16
                self.chan_count[op.chan] = c
                op.tick = c
            elif op.inc:
                self.eng_count[op.eng] += 1
                op.tick = self.eng_count[op.eng]
        chans = sorted(self.chan_count.keys(), key=str)
        engines = ["sync", "scalar", "gpsimd", "vector", "tensor"]
        per_eng = {e: [] for e in engines}
        for op in ops:
            per_eng[op.eng].append(op)
        with contextlib.ExitStack() as st:
            sems = {}
            for e in COMPUTE:
                sems[("e", e)] = st.enter_context(nc.semaphore("s_" + e))
            for i, c in enumerate(chans):
                sems[("c", c)] = st.enter_context(nc.semaphore("d%d" % i))
            block = st.enter_context(nc.Block())

            def make(ename):
                def body(eng):
                    seen = {}
                    for op in per_eng[ename]:
                        need = {}
                        for j in op.waits:
                            d = ops[j]
                            key = ("c", d.chan) if d.dma else ("e", d.eng)
                            if d.tick > need.get(key, 0):
                                need[key] = d.tick
                        for key, val in need.items():
                            if seen.get(key, 0) >= val:
                                continue
                            eng.wait_ge(sems[key], val)
                            seen[key] = val
                        ins = op.fn(eng)
                        if op.dma:
                            ins.then_inc(sems[("c", op.chan)], 16)
                        elif op.inc:
                            ins.then_inc(sems[("e", op.eng)], 1)
                    for c in chans:
                        eng.wait_ge(sems[("c", c)], self.chan_count[c])
                    for e2 in COMPUTE:
                        if self.eng_count[e2] > 0:
                            eng.wait_ge(sems[("e", e2)], self.eng_count[e2])
                return body

            for ename in engines:
                getattr(block, ename)(make(ename))


def _mk_ident(P, identf, ident):
    P.pool(lambda e: e.memset(identf[:], 0.0), writes=["identf"])
    P.pool(lambda e: e.affine_select(out=identf[:], in_=identf[:], pattern=[[-1, 128]],
                                     compare_op=ALU.not_equal, fill=1.0, base=0,
                                     channel_multiplier=1),
           reads=["identf"], writes=["identf"])
    P.dve(lambda e: e.tensor_copy(out=ident[:], in_=identf[:]), reads=["identf"], writes=["ident"])


def _rstd(P, out_ap, in_ap, inv_n, rkeys, wkey, tmp_ap, tmpkey):
    P.act(lambda e: e.activation(out=tmp_ap, in_=in_ap, func=AF.Ln, scale=inv_n, bias=EPS),
          reads=rkeys, writes=[tmpkey])
    P.act(lambda e: e.activation(out=out_ap, in_=tmp_ap, func=AF.Exp, scale=-0.5),
          reads=[tmpkey], writes=[wkey])


def phase_A(nc, T):
    with contextlib.ExitStack() as st:
        def sb(name, shape, dt):
            return st.enter_context(nc.sbuf_tensor(name, shape, dt))

        GB = 9
        uT = sb("uT", [128, 16, GB * 128], BF16)
        xt = [sb("xt%d" % i, [128, 2048], F32) for i in range(2)]
        ub = [sb("ub%d" % i, [128, 2048], BF16) for i in range(2)]
        junk = sb("junk", [128, 2048], BF16)
        wp = [sb("wp%d" % i, [128, 16, 512], BF16) for i in range(2)]
        gmix = sb("gmix", [128, 2048], F32)
        ck = sb("ck_s", [128, 33, 32], F32)
        sk = sb("sk_s", [128, 33, 32], F32)
        cq = sb("cq_s", [128, 16, 32], F32)
        sq = sb("sq_s", [128, 16, 32], F32)
        graw = sb("graw", [128, 2, 64], F32)
        gq = sb("gq", [128, 8, 64], F32)
        gk = sb("gk", [128, 8, 64], F32)
        ident = sb("ident", [128, 128], BF16)
        identf = sb("identf", [128, 128], F32)
        stat = sb("stat", [128, 16], F32)
        nst = sb("nst", [128, 2, 24], F32)
        t1 = [sb("t1_%d" % i, [128, 8, 64], F32) for i in range(2)]
        t2 = [sb("t2_%d" % i, [128, 8, 64], F32) for i in range(2)]
        rt = [sb("rt%d" % i, [128, 4, 8, 32], F32) for i in range(2)]
        yrot = [sb("yrot%d" % i, [128, 8, 64], BF16) for i in range(2)]
        kst = [sb("kst%d" % i, [128, 4, 512], BF16) for i in range(2)]
        vst = [sb("vst%d" % i, [128, 4, 4, 128], BF16) for i in range(2)]
        sst = [sb("sst%d" % i, [128, 512], BF16) for i in range(2)]
        ps = [st.enter_context(nc.psum_tensor("psA%d" % i, [128, 512], F32)) for i in range(8)]

        P = Prog(nc)
        _mk_ident(P, identf, ident)
        P.dma("sync", "gmix", gmix[:], T["gmix_b"], writes=["gmix"])
        P.dma("sync", "ck", ck[:], T["ck"], writes=["ck"])
        P.dma("sync", "sk", sk[:], T["sk"], writes=["sk"])
        P.dma("sync", "cq", cq[:], T["cq"], writes=["cq"])
        P.dma("sync", "sq", sq[:], T["sq"], writes=["sq"])
        P.dma("sync", "graw", graw[:], T["qkg_b"], writes=["graw"])
        for g in range(8):
            P.dve(lambda e, g=g: e.tensor_scalar(out=gq[:, g, :], in0=graw[:, 0, :], scalar1=0.125,
                                                 scalar2=None, op0=ALU.mult),
                  reads=["graw"], writes=["gq"])
            P.dve(lambda e, g=g: e.tensor_copy(out=gk[:, g, :], in_=graw[:, 1, :]),
                  reads=["graw"], writes=["gk"])

        w_in = T["w_in"]
        groups = [("k", list(range(0, 9))), ("k", list(range(9, 17))), ("k", list(range(17, 25))),
                  ("k", list(range(25, 33))), ("q", list(range(0, 8))), ("q", list(range(8, 16)))]
        pieces = []
        for gi, (kind, blks) in enumerate(groups):
            if kind == "k":
                for job in ("dk", "dv", "sk", "sv"):
                    for j in range(2):
                        pieces.append((gi, job, j))
            else:
                for job in ("dq", "sq"):
                    for j in range(2):
                        pieces.append((gi, job, j))
        colbase = {"dq": 0, "dk": 1024, "dv": 2048, "sq": 3072, "sk": 4096, "sv": 5120}

        def load_piece(n):
            gi, job, j = pieces[n]
            c0 = colbase[job] + 512 * j
            src = w_in[:, c0:c0 + 512].rearrange("(c p) n -> p c n", p=128)
            P.dma("gpsimd", "wp%d" % (n % 2), wp[n % 2][:], src, writes=["wp%d" % (n % 2)])

        cnt = {"x": 0, "mm": 0, "nr": 0, "st": 0, "vs": 0, "ss": 0, "ev": 0}

        def prologue(kind, blks):
            src = T["xk"] if kind == "k" else T["xq"]
            for bi, blk in enumerate(blks):
                n = cnt["x"]
                cnt["x"] += 1
                s2 = n % 2
                P.dma("sync", "xt%d" % s2, xt[s2][:], src[blk * 128:(blk + 1) * 128, :],
                      writes=["xt%d" % s2])
                sc = stat[:, 4 * s2:4 * s2 + 1]
                P.act(lambda e, s2=s2, sc=sc: e.activation(out=junk[:], in_=xt[s2][:], func=AF.Square,
                                                           accum_out=sc),
                      reads=["xt%d" % s2], writes=["junk", "st_a%d" % s2])
                _rstd(P, stat[:, 4 * s2 + 2:4 * s2 + 3], sc, 1.0 / D, ["st_a%d" % s2], "st_c%d" % s2,
                      stat[:, 4 * s2 + 1:4 * s2 + 2], "st_b%d" % s2)
                P.dve(lambda e, s2=s2: e.scalar_tensor_tensor(
                    out=ub[s2][:], in0=xt[s2][:], scalar=stat[:, 4 * s2 + 2:4 * s2 + 3], in1=gmix[:],
                    op0=ALU.mult, op1=ALU.mult),
                    reads=["xt%d" % s2, "st_c%d" % s2, "gmix"], writes=["ub%d" % s2])
                for c in range(16):
                    bank = 2 * s2 + c // 8
                    pv = ps[bank][:].bitcast(BF16)
                    P.pe(lambda e, c=c, pv=pv, s2=s2: e.transpose(
                        out=pv[:, (c % 8) * 128:(c % 8 + 1) * 128], in_=ub[s2][:, c * 128:(c + 1) * 128],
                        identity=ident[:]),
                        reads=["ub%d" % s2, "ident"], writes=["psA%d" % bank])
                for hh in range(2):
                    bank = 2 * s2 + hh
                    pv = ps[bank][:].bitcast(BF16).rearrange("p (c t) -> p c t", c=8)
                    dst = uT[:, hh * 8:(hh + 1) * 8, bi * 128:(bi + 1) * 128]
                    if hh == 0:
                        P.act(lambda e, pv=pv, dst=dst: e.copy(out=dst, in_=pv),
                              reads=["psA%d" % bank], writes=[("uT", bi, hh)])
                    else:
                        P.dve(lambda e, pv=pv, dst=dst: e.tensor_copy(out=dst, in_=pv),
                              reads=["psA%d" % bank], writes=[("uT", bi, hh)])

        def mm_bank():
            b = 4 + cnt["mm"] % 3
            cnt["mm"] += 1
            return b

        def tokmajor_mm(bi, slot, bank):
            for c in range(16):
                P.pe(lambda e, c=c, bi=bi, slot=slot, bank=bank: e.matmul(
                    ps[bank][:], lhsT=uT[:, c, bi * 128:(bi + 1) * 128], rhs=wp[slot][:, c, :],
                    start=(c == 0), stop=(c == 15)),
                    reads=[("uT", bi, 0), ("uT", bi, 1), "wp%d" % slot], writes=["psA%d" % bank])

        def job_plain(kind, blks, slot, dst, h0):
            nb = len(blks)
            for bi, blk in enumerate(blks):
                bank = mm_bank()
                tokmajor_mm(bi, slot, bank)
                sg = bi // 4
                vs = cnt["vs"] % 2
                dstv = vst[vs][:, :, bi % 4, :]
                srcv = ps[bank][:].rearrange("p (h d) -> p h d", h=4)
                ev = cnt["ev"]
                cnt["ev"] += 1
                if ev % 2 == 0:
                    P.act(lambda e, dstv=dstv, srcv=srcv: e.copy(out=dstv, in_=srcv),
                          reads=["psA%d" % bank], writes=["vst%d" % vs])
                else:
                    P.dve(lambda e, dstv=dstv, srcv=srcv: e.tensor_copy(out=dstv, in_=srcv),
                          reads=["psA%d" % bank], writes=["vst%d" % vs])
                if bi % 4 == 3 or bi == nb - 1:
                    n_in = bi % 4 + 1
                    b0 = blks[bi - (bi % 4)]
                    d_ap = dst[h0:h0 + 4, :, b0:b0 + n_in, :].rearrange("h p b d -> p h b d")
                    P.dma("sync", "vst%d" % vs, d_ap, vst[vs][:, :, 0:n_in, :],
                          reads=["vst%d" % vs], writes=[("scr", id(dst), h0, b0)])
                    cnt["vs"] += 1

        def job_normrope(kind, blks, slot, dst, h0):
            nb = len(blks)
            gain = gk if kind == "k" else gq
            ctab = ck if kind == "k" else cq
            stab = sk if kind == "k" else sq
            for bi, blk in enumerate(blks):
                bank = mm_bank()
                tokmajor_mm(bi, slot, bank)
                r = cnt["nr"] % 2
                cnt["nr"] += 1
                bk = "psA%d" % bank
                pv3 = ps[bank][:].rearrange("p (g d) -> p g d", g=8)
                P.act(lambda e, r=r, pv3=pv3: e.activation(out=t1[r][:], in_=pv3, func=AF.Square),
                      reads=[bk], writes=["t1_%d" % r])
                P.dve(lambda e, r=r: e.tensor_reduce(out=nst[:, r, 0:8], in_=t1[r][:], axis=AX.X, op=ALU.add),
                      reads=["t1_%d" % r], writes=["nsa%d" % r])
                _rstd(P, nst[:, r, 16:24], nst[:, r, 0:8], 1.0 / 64, ["nsa%d" % r], "nsc%d" % r,
                      nst[:, r, 8:16], "nsb%d" % r)
                P.dve(lambda e, r=r, pv3=pv3: e.tensor_tensor(
                    out=t2[r][:], in0=pv3, in1=nst[:, r, 16:24].unsqueeze(2).to_broadcast([128, 8, 64]),
                    op=ALU.mult),
                    reads=[bk, "nsc%d" % r], writes=["t2_%d" % r])
                P.dve(lambda e, r=r, gain=gain: e.tensor_tensor(out=t1[r][:], in0=t2[r][:], in1=gain[:],
                                                                op=ALU.mult),
                      reads=["t2_%d" % r, "gq", "gk"], writes=["t1_%d" % r])
                cb = ctab[:, blk, :].unsqueeze(1).to_broadcast([128, 8, 32])
                sbb = stab[:, blk, :].unsqueeze(1).to_broadcast([128, 8, 32])
                x1 = t1[r][:, :, 0:32]
                x2 = t1[r][:, :, 32:64]
                tabs = ["ck", "sk", "cq", "sq"]
                P.dve(lambda e, r=r, x1=x1, cb=cb: e.tensor_tensor(out=rt[r][:, 0], in0=x1, in1=cb, op=ALU.mult),
                      reads=["t1_%d" % r] + tabs, writes=[("rt", r, 0)])
                P.dve(lambda e, r=r, x2=x2, sbb=sbb: e.tensor_tensor(out=rt[r][:, 1], in0=x2, in1=sbb, op=ALU.mult),
                      reads=["t1_%d" % r] + tabs, writes=[("rt", r, 1)])
                P.dve(lambda e, r=r, x2=x2, cb=cb: e.tensor_tensor(out=rt[r][:, 2], in0=x2, in1=cb, op=ALU.mult),
                      reads=["t1_%d" % r] + tabs, writes=[("rt", r, 2)])
                P.dve(lambda e, r=r, x1=x1, sbb=sbb: e.tensor_tensor(out=rt[r][:, 3], in0=x1, in1=sbb, op=ALU.mult),
                      reads=["t1_%d" % r] + tabs, writes=[("rt", r, 3)])
                P.dve(lambda e, r=r: e.tensor_tensor(out=yrot[r][:, :, 0:32], in0=rt[r][:, 0], in1=rt[r][:, 1],
                                                     op=ALU.subtract),
                      reads=[("rt", r, 0), ("rt", r, 1)], writes=[("yrot", r, 0)])
                P.dve(lambda e, r=r: e.tensor_tensor(out=yrot[r][:, :, 32:64], in0=rt[r][:, 2], in1=rt[r][:, 3],
                                                     op=ALU.add),
                      reads=[("rt", r, 2), ("rt", r, 3)], writes=[("yrot", r, 1)])
                pv = ps[7][:].bitcast(BF16)
                yflat = yrot[r][:].rearrange("p g d -> p (g d)")
                for hh in range(4):
                    P.pe(lambda e, hh=hh, pv=pv, yflat=yflat: e.transpose(
                        out=pv[:, hh * 128:(hh + 1) * 128], in_=yflat[:, hh * 128:(hh + 1) * 128],
                        identity=ident[:]),
                        reads=[("yrot", r, 0), ("yrot", r, 1), "ident"], writes=["psA7"])
                ks = cnt["st"] % 2
                dstv = kst[ks][:, :, (bi % 4) * 128:(bi % 4 + 1) * 128]
                srcv = pv[:, 0:512].rearrange("p (h t) -> p h t", h=4)
                P.act(lambda e, dstv=dstv, srcv=srcv: e.copy(out=dstv, in_=srcv),
                      reads=["psA7"], writes=["kst%d" % ks])
                if bi % 4 == 3 or bi == nb - 1:
                    n_in = bi % 4 + 1
                    b0 = blks[bi - (bi % 4)]
                    d_ap = dst[h0:h0 + 4, :, b0 * 128:(b0 + n_in) * 128].rearrange("h p t -> p h t")
                    P.dma("sync", "kst%d" % ks, d_ap, kst[ks][:, :, 0:n_in * 128],
                          reads=["kst%d" % ks], writes=[("scr", id(dst), h0, b0)])
                    cnt["st"] += 1

        def job_transposed(kind, blks, slot, dst, h0, scale):
            ntok = len(blks) * 128
            tok0 = blks[0] * 128
            for hh in range(4):
                t0 = 0
                while t0 < ntok:
                    n = min(512, ntok - t0)
                    bank = mm_bank()
                    keys = []
                    for bb in range(t0 // 128, (t0 + n) // 128):
                        keys += [("uT", bb, 0), ("uT", bb, 1)]
                    for c in range(16):
                        P.pe(lambda e, c=c, hh=hh, t0=t0, n=n, bank=bank, slot=slot: e.matmul(
                            ps[bank][:, 0:n], lhsT=wp[slot][:, c, hh * 128:(hh + 1) * 128],
                            rhs=uT[:, c, t0:t0 + n], start=(c == 0), stop=(c == 15)),
                            reads=keys + ["wp%d" % slot], writes=["psA%d" % bank])
                    s2 = cnt["ss"] % 2
                    cnt["ss"] += 1
                    ev = cnt["ev"]
                    cnt["ev"] += 1
                    if scale != 1.0 or ev % 2 == 0:
                        P.act(lambda e, s2=s2, n=n, bank=bank: e.activation(
                            out=sst[s2][:, 0:n], in_=ps[bank][:, 0:n], func=AF.Copy, scale=scale),
                            reads=["psA%d" % bank], writes=["sst%d" % s2])
                    else:
                        P.dve(lambda e, s2=s2, n=n, bank=bank: e.tensor_copy(out=sst[s2][:, 0:n],
                                                                             in_=ps[bank][:, 0:n]),
                              reads=["psA%d" % bank], writes=["sst%d" % s2])
                    P.dma("sync", "sst%d" % s2, dst[h0 + hh, :, tok0 + t0:tok0 + t0 + n], sst[s2][:, 0:n],
                          reads=["sst%d" % s2], writes=[("scr", id(dst), h0 + hh, tok0 + t0)])
                    t0 += n

        load_piece(0)
        cur_group = -1
        for n, (gi, job, j) in enumerate(pieces):
            kind, blks = groups[gi]
            if gi != cur_group:
                prologue(kind, blks)
                cur_group = gi
            if n + 1 < len(pieces):
                load_piece(n + 1)
            slot = n % 2
            h0 = 4 * j
            if job == "dk":
                job_normrope("k", blks, slot, T["kd"], h0)
            elif job == "dq":
                job_normrope("q", blks, slot, T["qd"], h0)
            elif job == "dv":
                job_plain("k", blks, slot, T["vd"], h0)
            elif job == "sv":
                job_plain("k", blks, slot, T["vs"], h0)
            elif job == "sk":
                job_transposed("k", blks, slot, T["ks"], h0, 1.0)
            elif job == "sq":
                job_transposed("q", blks, slot, T["qs"], h0, 128.0 ** -0.5)
        P.emit()


def phase_B(nc, T, mixT, debug=False):
    with contextlib.ExitStack() as st:
        def sb(name, shape, dt):
            return st.enter_context(nc.sbuf_tensor(name, shape, dt))

        KT = [sb("KT%d" % i, [128, L], BF16) for i in range(2)]
        V = [sb("V%d" % i, [128, NB, 128], BF16) for i in range(2)]
        QT = [sb("QT%d" % i, [128, TOWN], BF16) for i in range(2)]
        maskf = sb("maskf", [128, 4, 128], F32)
        maskb = sb("maskb", [128, 4, 128], BF16)
        onesf = sb("onesf", [128, 128], F32)
        ones = sb("ones", [128, 128], BF16)
        ones_m = sb("ones_m", [128, 128], BF16)
        tri = sb("tri", [128, 128], BF16)
        tri_m = sb("tri_m", [128, 128], BF16)
        lamb = sb("lamb", [128, 4, 64], F32)
        lt = sb("lt", [128, 2, 64], F32)
        ls = sb("ls", [128, 8], F32)
        og = sb("og", [128, 2], F32)
        gcol = sb("gcol", [128, 2], F32)
        Pb = [sb("Pb%d" % i, [128, 2, 512], BF16) for i in range(3)]
        eb = [sb("eb%d" % i, [128, 512], F32) for i in range(2)]
        spb = [sb("spb%d" % i, [128, 512], BF16) for i in range(2)]
        wb = [sb("wb%d" % i, [128, 512], F32) for i in range(2)]
        Ab = [sb("Ab%d" % i, [128, 512], BF16) for i in range(2)]
        acc = [sb("acc%d" % i, [128, 512], BF16) for i in range(2)]
        R1 = sb("R1", [128, 512], F32)
        T1 = sb("T1", [128, 512], F32)
        T2 = sb("T2", [128, 512], F32)
        ob = sb("ob", [128, 512], F32)
        sqb = sb("sqb", [128, 512], BF16)
        lnv = sb("lnv", [128, 512], F32)
        rsb = sb("rsb", [128, 512], F32)
        Dp = [st.enter_context(nc.psum_tensor("psB%d" % i, [128, 1024], F32)) for i in range(4)]

        P = Prog(nc)
        P.dma("sync", "maskf", maskf[:], T["masks"], writes=["maskf"])
        P.dve(lambda e: e.tensor_copy(out=maskb[:], in_=maskf[:]), reads=["maskf"], writes=["maskb"])
        P.pool(lambda e: e.memset(onesf[:], 1.0), writes=["onesf"])
        P.dve(lambda e: e.tensor_copy(out=ones[:], in_=onesf[:]), reads=["onesf"], writes=["ones"])
        P.pool(lambda e: e.affine_select(out=onesf[:], in_=onesf[:], pattern=[[0, 128]], compare_op=ALU.is_ge,
                                         fill=0.0, base=-PAD, channel_multiplier=1),
               reads=["ones", "onesf"], writes=["onesf"])
        P.dve(lambda e: e.tensor_copy(out=ones_m[:], in_=onesf[:]), reads=["onesf"], writes=["ones_m"])
        P.pool(lambda e: e.memset(onesf[:], 1.0), reads=["ones_m"], writes=["onesf"])
        P.pool(lambda e: e.affine_select(out=onesf[:], in_=onesf[:], pattern=[[-1, 128]], compare_op=ALU.is_ge,
                                         fill=0.0, base=0, channel_multiplier=1),
               reads=["onesf"], writes=["onesf"])
        P.dve(lambda e: e.tensor_copy(out=tri[:], in_=onesf[:]), reads=["onesf"], writes=["tri"])
        P.pool(lambda e: e.affine_select(out=onesf[:], in_=onesf[:], pattern=[[0, 128]], compare_op=ALU.is_ge,
                                         fill=0.0, base=-PAD, channel_multiplier=1),
               reads=["tri", "onesf"], writes=["onesf"])
        P.dve(lambda e: e.tensor_copy(out=tri_m[:], in_=onesf[:]), reads=["onesf"], writes=["tri_m"])
        P.dma("sync", "lamb", lamb[:], T["lam_b"], writes=["lamb"])
        P.dve(lambda e: e.tensor_tensor(out=lt[:, 0, :], in0=lamb[:, 0, :], in1=lamb[:, 1, :], op=ALU.mult),
              reads=["lamb"], writes=["lt0"])
        P.dve(lambda e: e.tensor_tensor(out=lt[:, 1, :], in0=lamb[:, 2, :], in1=lamb[:, 3, :], op=ALU.mult),
              reads=["lamb"], writes=["lt1"])
        P.dve(lambda e: e.tensor_reduce(out=ls[:, 0:2], in_=lt[:], axis=AX.X, op=ALU.add),
              reads=["lt0", "lt1"], writes=["ls_a"])
        P.act(lambda e: e.activation(out=ls[:, 2:4], in_=ls[:, 0:2], func=AF.Exp), reads=["ls_a"], writes=["ls_b"])
        P.dve(lambda e: e.tensor_tensor(out=ls[:, 4:5], in0=ls[:, 3:4], in1=ls[:, 2:3], op=ALU.subtract),
              reads=["ls_b"], writes=["ls_c"])
        P.dve(lambda e: e.tensor_scalar(out=ls[:, 5:6], in0=ls[:, 4:5], scalar1=-LAMBDA_INIT, scalar2=None,
                                        op0=ALU.add),
              reads=["ls_c"], writes=["neg_lam"])
        neg_lam = ls[:, 5:6]
        P.dma("sync", "og", og[:], T["og_col"], writes=["og"])
        P.dve(lambda e: e.tensor_scalar(out=gcol[:, 0:1], in0=og[:, 0:1], scalar1=1.0 - LAMBDA_INIT, scalar2=None,
                                        op0=ALU.mult),
              reads=["og"], writes=["gcol0"])
        P.dve(lambda e: e.tensor_copy(out=gcol[:, 1:2], in_=og[:, 1:2]), reads=["og"], writes=["gcol1"])

        def load_head(hd):
            s = hd % 2
            if hd < 8:
                ksrc, vsrc, qsrc, h = T["kd"], T["vd"], T["qd"], hd
            else:
                ksrc, vsrc, qsrc, h = T["ks"], T["vs"], T["qs"], hd - 8
            P.dma("sync", "KT%d" % s, KT[s][:], ksrc[h], writes=["KT%d" % s])
            P.dma("sync", "V%d" % s, V[s][:], vsrc[h], writes=["V%d" % s])
            P.dma("sync", "QT%d" % s, QT[s][:], qsrc[h], writes=["QT%d" % s])

        def step_geom(i, kb):
            jmin = max(0, (kb - 1) // 2 - 4 * i) if kb >= 1 else 0
            c0 = 128 * jmin
            masked = kb >= 8 * i + 1
            mi = 0 if (kb % 2 == 1) else 1
            return c0, masked, mi

        cntp = {"p": 0}

        def head_norm(hd, i, src_ap, src_keys, ss_ap, ss_key, gi):
            q0 = 512 * i
            P.act(lambda e: e.activation(out=sqb[:], in_=src_ap, func=AF.Square), reads=src_keys, writes=["sqb"])
            P.pe(lambda e: e.matmul(ss_ap, lhsT=ones[:], rhs=sqb[:], start=True, stop=True),
                 reads=["sqb", "ones"], writes=[ss_key])
            _rstd(P, rsb[:], ss_ap, 1.0 / 128, [ss_key], "rsb", lnv[:], "lnv")
            P.dve(lambda e: e.scalar_tensor_tensor(out=mixT[:, hd, q0:q0 + 512], in0=src_ap,
                                                   scalar=gcol[:, gi:gi + 1], in1=rsb[:],
                                                   op0=ALU.mult, op1=ALU.mult),
                  reads=src_keys + ["rsb", "gcol%d" % gi], writes=[("mixT", hd, i)])

        def diff_head(hd):
            s = hd % 2
            kt, v, qt = KT[s], V[s], QT[s]
            hk = ["KT%d" % s, "QT%d" % s]
            O1, L1 = Dp[2][:, 0:512], Dp[2][:, 512:1024]
            O2, L2 = Dp[3][:, 0:512], Dp[3][:, 512:1024]

            def do_tile(i):
                q0 = 512 * i
                nk = 8 * i + 9
                steps = list(range(nk))

                def qk(kb):
                    c0, _, _ = step_geom(i, kb)
                    d = kb % 2
                    for c in range(2):
                        P.pe(lambda e, c=c, kb=kb, c0=c0, d=d: e.matmul(
                            Dp[d][:, c * 512 + c0:c * 512 + 512],
                            lhsT=kt[c * 64:(c + 1) * 64, kb * 128:(kb + 1) * 128],
                            rhs=qt[c * 64:(c + 1) * 64, q0 + c0:q0 + 512], start=True, stop=True),
                            reads=hk, writes=[("S", d, c)])

                qk(0)
                for kb in steps:
                    if kb + 1 < nk:
                        qk(kb + 1)
                    c0, masked, mi = step_geom(i, kb)
                    d = kb % 2
                    pslot = cntp["p"] % 3
                    cntp["p"] += 1
                    pb = Pb[pslot]
                    pk = "Pb%d" % pslot
                    sv = Dp[d][:].rearrange("p (c n) -> p c n", c=2)[:, :, c0:512]
                    P.act(lambda e, pb=pb, sv=sv, c0=c0: e.activation(out=pb[:, :, c0:512], in_=sv, func=AF.Exp),
                          reads=[("S", d, 0), ("S", d, 1)], writes=[pk])
                    if masked:
                        mk = maskb[:, mi, :].unsqueeze(1).to_broadcast([128, 2, 128])
                        P.dve(lambda e, pb=pb, c0=c0, mk=mk: e.tensor_tensor(
                            out=pb[:, :, c0:c0 + 128], in0=pb[:, :, c0:c0 + 128], in1=mk, op=ALU.mult),
                            reads=[pk, "maskb"], writes=[pk])
                    first = (kb == 0)
                    lastk = (kb == nk - 1)
                    onem = ones_m if kb == 0 else ones
                    for c, (Oc, Lc, ok, lk) in enumerate(((O1, L1, "O1", "L1"), (O2, L2, "O2", "L2"))):
                        P.pe(lambda e, c=c, Oc=Oc, pb=pb, c0=c0, kb=kb, first=first, lastk=lastk: e.matmul(
                            Oc[:, c0:512], lhsT=v[:, kb, :], rhs=pb[:, c, c0:512], start=first, stop=lastk),
                            reads=[pk, "V%d" % s], writes=[ok])
                        P.pe(lambda e, c=c, Lc=Lc, pb=pb, c0=c0, onem=onem, first=first, lastk=lastk: e.matmul(
                            Lc[:, c0:512], lhsT=onem[:], rhs=pb[:, c, c0:512], start=first, stop=lastk),
                            reads=[pk, "ones", "ones_m"], writes=[lk])
                P.dve(lambda e: e.reciprocal(out=R1[:], in_=L1), reads=["L1"], writes=["R1"])
                P.dve(lambda e: e.tensor_tensor(out=T1[:], in0=O1, in1=R1[:], op=ALU.mult),
                      reads=["O1", "R1"], writes=["T1"])
                P.dve(lambda e: e.reciprocal(out=R1[:], in_=L2), reads=["L2", "T1"], writes=["R1"])
                P.dve(lambda e: e.tensor_tensor(out=T2[:], in0=O2, in1=R1[:], op=ALU.mult),
                      reads=["O2", "R1"], writes=["T2"])
                P.dve(lambda e: e.scalar_tensor_tensor(out=ob[:], in0=T2[:], scalar=neg_lam, in1=T1[:],
                                                       op0=ALU.mult, op1=ALU.add),
                      reads=["T1", "T2", "neg_lam"], writes=["ob"])
                head_norm(hd, i, ob[:], ["ob"], L1, "L1", 0)

            for i in range(4):
                do_tile(i)

        def sb_head(hd):
            s = hd % 2
            kt, v, qt = KT[s], V[s], QT[s]
            hk = ["KT%d" % s, "QT%d" % s]

            def do_tile(i):
                q0 = 512 * i
                nk = 8 * i + 9
                order = list(range(nk - 1, -1, -1))
                a = acc[i % 2]
                ak = "acc%d" % (i % 2)
                Ob = Dp[2 + i % 2][:, 0:512]
                Ok = "SO%d" % (i % 2)
                SSb = Dp[2 + i % 2][:, 512:1024]
                SSk = "SS%d" % (i % 2)
                P.pool(lambda e, a=a: e.memset(a[:], 0.0), writes=[ak])

                def zmm(t):
                    kb = order[t]
                    c0, _, _ = step_geom(i, kb)
                    d = t % 2
                    P.pe(lambda e, kb=kb, c0=c0, d=d: e.matmul(
                        Dp[0][:, d * 512 + c0:d * 512 + 512], lhsT=kt[:, kb * 128:(kb + 1) * 128],
                        rhs=qt[:, q0 + c0:q0 + 512], start=True, stop=True),
                        reads=hk, writes=[("Z", d)])

                def stage_a(t):
                    kb = order[t]
                    c0, masked, mi = step_geom(i, kb)
                    d = t % 2
                    P.act(lambda e, d=d, c0=c0: e.activation(out=eb[d][:, c0:512],
                                                             in_=Dp[0][:, d * 512 + c0:d * 512 + 512], func=AF.Exp),
                          reads=[("Z", d)], writes=["eb%d" % d])
                    P.act(lambda e, d=d, c0=c0: e.activation(out=spb[d][:, c0:512], in_=eb[d][:, c0:512],
                                                             func=AF.Ln, bias=1.0),
                          reads=["eb%d" % d], writes=["spb%d" % d])
                    if masked:
                        P.dve(lambda e, d=d, c0=c0, mi=mi: e.tensor_tensor(
                            out=spb[d][:, c0:c0 + 128], in0=spb[d][:, c0:c0 + 128], in1=maskb[:, 2 + mi, :],
                            op=ALU.mult),
                            reads=["spb%d" % d, "maskb"], writes=["spb%d" % d])
                    tr = tri_m if kb == 0 else tri
                    only = (t == 0)
                    P.pe(lambda e, d=d, c0=c0, tr=tr, only=only: e.matmul(
                        Dp[1][:, d * 512 + c0:d * 512 + 512], lhsT=tr[:], rhs=spb[d][:, c0:512],
                        start=True, stop=only),
                        reads=["spb%d" % d, "tri", "tri_m"], writes=[("C", d)])
                    if not only:
                        P.pe(lambda e, d=d, c0=c0, a=a: e.matmul(
                            Dp[1][:, d * 512 + c0:d * 512 + 512], lhsT=ones[:], rhs=a[:, c0:512],
                            start=False, stop=True),
                            reads=[ak, "ones"], writes=[("C", d)])
                    if t != nk - 1:
                        P.pool(lambda e, d=d, c0=c0, a=a: e.tensor_tensor(
                            out=a[:, c0:512], in0=a[:, c0:512], in1=spb[d][:, c0:512], op=ALU.add),
                            reads=[ak, "spb%d" % d], writes=[ak])

                def stage_b(t):
                    kb = order[t]
                    c0, masked, mi = step_geom(i, kb)
                    d = t % 2
                    P.act(lambda e, d=d, c0=c0: e.activation(out=wb[d][:, c0:512],
                                                             in_=Dp[1][:, d * 512 + c0:d * 512 + 512],
                                                             func=AF.Exp, scale=-1.0),
                          reads=[("C", d)], writes=["wb%d" % d])
                    P.dve(lambda e, d=d, c0=c0: e.tensor_tensor(out=Ab[d][:, c0:512], in0=eb[d][:, c0:512],
                                                                in1=wb[d][:, c0:512], op=ALU.mult),
                          reads=["eb%d" % d, "wb%d" % d], writes=["Ab%d" % d])
                    if masked:
                        P.dve(lambda e, d=d, c0=c0, mi=mi: e.tensor_tensor(
                            out=Ab[d][:, c0:c0 + 128], in0=Ab[d][:, c0:c0 + 128], in1=maskb[:, 2 + mi, :],
                            op=ALU.mult),
                            reads=["Ab%d" % d, "maskb"], writes=["Ab%d" % d])
                    P.pe(lambda e, d=d, c0=c0, kb=kb, t=t: e.matmul(
                        Ob[:, c0:512], lhsT=v[:, kb, :], rhs=Ab[d][:, c0:512],
                        start=(t == 0), stop=(t == nk - 1), skip_group_check=True),
                        reads=["Ab%d" % d, "V%d" % s], writes=[Ok])

                zmm(0)
                for t in range(nk):
                    if t + 1 < nk:
                        zmm(t + 1)
                    stage_a(t)
                    if t >= 1:
                        stage_b(t - 1)
                stage_b(nk - 1)
                head_norm(hd, i, Ob, [Ok], SSb, SSk, 1)

            for i in range(4):
                do_tile(i)

        load_head(0)
        for hd in range(16):
            if hd + 1 < 16:
                load_head(hd + 1)
            if hd < 8:
                diff_head(hd)
            else:
                sb_head(hd)
        if debug:
            P.dma("sync", "mixdump", T["mixdump"], mixT[:],
                  reads=[("mixT", hd, i) for hd in range(16) for i in range(4)], writes=["mixdump"])
        P.emit()


def phase_C(nc, T, mixT, rstd2, debug=False, hasB=True):
    with contextlib.ExitStack() as st:
        def sb(name, shape, dt):
            return st.enter_context(nc.sbuf_tensor(name, shape, dt))

        wo = [sb("wo%d" % i, [128, 16, 512], BF16) for i in range(2)]
        xr = [sb("xr%d" % i, [128, 512], F32) for i in range(3)]
        h1t = [sb("h1t%d" % i, [128, 512], F32) for i in range(3)]
        junk = sb("junkC", [128, 512], BF16)
        ssq = sb("ssq", [128, 16, 4], F32)
        ssum = sb("ssum", [128, 16], F32)
        lnt = sb("lnt", [128, 16], F32)
        ps = [st.enter_context(nc.psum_tensor("psC%d" % i, [128, 512], F32)) for i in range(4)]
        P = Prog(nc)
        w_out = T["w_out"]

        def load_w(cg):
            src = w_out[:, cg * 512:(cg + 1) * 512].rearrange("(c p) n -> p c n", p=128)
            P.dma("gpsimd", "wo%d" % (cg % 2), wo[cg % 2][:], src, writes=["wo%d" % (cg % 2)])

        mixk = []
        if debug and not hasB:
            P.dma("sync", "mixin", mixT[:], T["mixdump"], writes=["mixin"])
            mixk = ["mixin"]
        load_w(0)
        n = 0
        for cg in range(4):
            if cg + 1 < 4:
                load_w(cg + 1)
            for tb in range(16):
                r = n % 3
                bank = n % 4
                n += 1
                P.dma("sync", "xr%d" % r, xr[r][:], T["xq"][tb * 128:(tb + 1) * 128, cg * 512:(cg + 1) * 512],
                      writes=["xr%d" % r])
                for c in range(16):
                    P.pe(lambda e, c=c, tb=tb, bank=bank, cg=cg: e.matmul(
                        ps[bank][:], lhsT=mixT[:, c, tb * 128:(tb + 1) * 128], rhs=wo[cg % 2][:, c, :],
                        start=(c == 0), stop=(c == 15)),
                        reads=["wo%d" % (cg % 2)] + mixk, writes=["psC%d" % bank])
                P.dve(lambda e, r=r, bank=bank: e.tensor_tensor(out=h1t[r][:], in0=ps[bank][:], in1=xr[r][:],
                                                                op=ALU.add),
                      reads=["psC%d" % bank, "xr%d" % r], writes=["h1t%d" % r])
                P.act(lambda e, r=r, tb=tb, cg=cg: e.activation(out=junk[:], in_=h1t[r][:], func=AF.Square,
                                                                accum_out=ssq[:, tb, cg:cg + 1]),
                      reads=["h1t%d" % r], writes=["junkC", ("ssq", tb, cg)])
                P.dma("sync", "h1t%d" % r, T["h1"][tb * 128:(tb + 1) * 128, cg * 512:(cg + 1) * 512], h1t[r][:],
                      reads=["h1t%d" % r], writes=[("h1", tb, cg)])
        allk = [("ssq", tb, cg) for tb in range(16) for cg in range(4)]
        P.dve(lambda e: e.tensor_reduce(out=ssum[:], in_=ssq[:], axis=AX.X, op=ALU.add), reads=allk, writes=["ssum"])
        _rstd(P, rstd2[:], ssum[:], 1.0 / D, ["ssum"], "rstd2", lnt[:], "lnt")
        if debug:
            P.dma("sync", "rstd2dump", T["rstd2dump"], rstd2[:], reads=["rstd2"], writes=["rstd2dump"])
        P.emit()


def phase_D(nc, T, rstd2, debug=False, hasC=True):
    with contextlib.ExitStack() as st:
        def sb(name, shape, dt):
            return st.enter_context(nc.sbuf_tensor(name, shape, dt))

        h1t = [sb("h1d%d" % i, [128, 2048], F32) for i in range(2)]
        hr = [sb("hr%d" % i, [128, 512], F32) for i in range(3)]
        gm = sb("gm", [128, 2048], F32)
        mb = [sb("mb%d" % i, [128, 2048], BF16) for i in range(2)]
        mT = sb("mT", [128, 16, 512], BF16)
        hidT = sb("hidT", [128, 64, 512], BF16)
        wu = [sb("wu%d" % i, [128, 16, 256], BF16) for i in range(3)]
        wd = [sb("wd%d" % i, [128, 8, 512], BF16) for i in range(3)]
        rl = [sb("rl%d" % i, [128, 512], BF16) for i in range(2)]
        ot = [sb("ot%d" % i, [128, 512], F32) for i in range(3)]
        ident = sb("identD", [128, 128], BF16)
        identf = sb("identfD", [128, 128], F32)
        ps = [st.enter_context(nc.psum_tensor("psD%d" % i, [128, 512], F32)) for i in range(8)]
        P = Prog(nc)
        _mk_ident(P, identf, ident)
        P.dma("sync", "gm", gm[:], T["gmlp_b"], writes=["gm"])
        rk = []
        if debug and not hasC:
            P.dma("sync", "rstd2in", rstd2[:], T["rstd2dump"], writes=["rstd2in"])
            rk = ["rstd2in"]
        w_up, w_down = T["w_up"], T["w_down"]

        sched = []
        for tt in range(4):
            for fp in range(32):
                sched.append(("u", tt, fp))
            for cg in range(4):
                for dp in range(8):
                    sched.append(("d", tt, cg, dp))
        ucnt = {"u": 0, "d": 0}
        slot_of = {}
        for it in sched:
            k = it[0]
            slot_of[it] = ucnt[k] % 3
            ucnt[k] += 1

        def issue(idx):
            it = sched[idx]
            s = slot_of[it]
            if it[0] == "u":
                fp = it[2]
                src = w_up[:, fp * 256:(fp + 1) * 256].rearrange("(c p) n -> p c n", p=128)
                P.dma("gpsimd", "wu%d" % s, wu[s][:], src, writes=["wu%d" % s])
            else:
                cg, dp = it[2], it[3]
                src = w_down[dp * 1024:(dp + 1) * 1024, cg * 512:(cg + 1) * 512].rearrange("(c p) n -> p c n", p=128)
                P.dma("gpsimd", "wd%d" % s, wd[s][:], src, writes=["wd%d" % s])

        AHEAD = 2
        nxt = {"i": 0}

        def ensure(idx):
            while nxt["i"] <= min(idx + AHEAD, len(sched) - 1):
                issue(nxt["i"])
                nxt["i"] += 1

        on = 0
        pos = 0
        for tt in range(4):
            for b in range(4):
                blk = tt * 4 + b
                m2 = b % 2
                P.dma("sync", "h1d%d" % m2, h1t[m2][:], T["h1"][blk * 128:(blk + 1) * 128, :],
                      writes=["h1d%d" % m2])
                P.dve(lambda e, b=b, blk=blk, m2=m2: e.scalar_tensor_tensor(
                    out=mb[m2][:], in0=h1t[m2][:], scalar=rstd2[:, blk:blk + 1], in1=gm[:],
                    op0=ALU.mult, op1=ALU.mult),
                    reads=["h1d%d" % m2, "gm"] + rk, writes=["mb%d" % m2])
                for c in range(16):
                    bank = 6 + c // 8
                    pv = ps[bank][:].bitcast(BF16)
                    P.pe(lambda e, c=c, pv=pv, m2=m2: e.transpose(
                        out=pv[:, (c % 8) * 128:(c % 8 + 1) * 128], in_=mb[m2][:, c * 128:(c + 1) * 128],
                        identity=ident[:]),
                        reads=["mb%d" % m2, "ident"], writes=["psD%d" % bank])
                for hh in range(2):
                    bank = 6 + hh
                    pv = ps[bank][:].bitcast(BF16).rearrange("p (c t) -> p c t", c=8)
                    dst = mT[:, hh * 8:(hh + 1) * 8, b * 128:(b + 1) * 128]
                    if hh == 0:
                        P.act(lambda e, pv=pv, dst=dst: e.copy(out=dst, in_=pv),
                              reads=["psD%d" % bank], writes=[("mT", b, hh)])
                    else:
                        P.dve(lambda e, pv=pv, dst=dst: e.tensor_copy(out=dst, in_=pv),
                              reads=["psD%d" % bank], writes=[("mT", b, hh)])
            mTk = [("mT", b, hh) for b in range(4) for hh in range(2)]
            for fp in range(32):
                ensure(pos)
                s = slot_of[sched[pos]]
                pos += 1
                for half in range(2):
                    fc = fp * 2 + half
                    bank = 4 + fc % 2
                    for c in range(16):
                        P.pe(lambda e, c=c, s=s, half=half, bank=bank: e.matmul(
                            ps[bank][:], lhsT=wu[s][:, c, half * 128:(half + 1) * 128], rhs=mT[:, c, :],
                            start=(c == 0), stop=(c == 15)),
                            reads=mTk + ["wu%d" % s], writes=["psD%d" % bank])
                    r = fc % 2
                    P.act(lambda e, r=r, bank=bank: e.activation(out=rl[r][:], in_=ps[bank][:], func=AF.Relu),
                          reads=["psD%d" % bank], writes=["rl%d" % r])
                    P.dve(lambda e, r=r, fc=fc: e.tensor_tensor(out=hidT[:, fc, :], in0=rl[r][:], in1=rl[r][:],
                                                                op=ALU.mult),
                          reads=["rl%d" % r], writes=[("hid", fc)])
            for cg in range(4):
                base = 0 if cg % 2 == 0 else 4
                for dp in range(8):
                    ensure(pos)
                    s = slot_of[sched[pos]]
                    pos += 1
                    for i8 in range(8):
                        fc = dp * 8 + i8
                        for tb in range(4):
                            P.pe(lambda e, fc=fc, i8=i8, tb=tb, s=s, base=base: e.matmul(
                                ps[base + tb][:], lhsT=hidT[:, fc, tb * 128:(tb + 1) * 128], rhs=wd[s][:, i8, :],
                                start=(fc == 0), stop=(fc == 63)),
                                reads=[("hid", fc), "wd%d" % s], writes=["psD%d" % (base + tb)])
                for tb in range(4):
                    o3 = on % 3
                    on += 1
                    blk = tt * 4 + tb
                    P.dma("sync", "hr%d" % o3, hr[o3][:], T["h1"][blk * 128:(blk + 1) * 128, cg * 512:(cg + 1) * 512],
                          writes=["hr%d" % o3])
                    P.dve(lambda e, o3=o3, tb=tb, cg=cg, base=base: e.tensor_tensor(
                        out=ot[o3][:], in0=ps[base + tb][:], in1=hr[o3][:], op=ALU.add),
                        reads=["psD%d" % (base + tb), "hr%d" % o3], writes=["ot%d" % o3])
                    P.dma("sync", "ot%d" % o3, T["y"][blk * 128:(blk + 1) * 128, cg * 512:(cg + 1) * 512], ot[o3][:],
                          reads=["ot%d" % o3], writes=[("y", blk, cg)])
        P.emit()


def build_nc(debug=False, phases="ABCD"):
    nc = bass.Bass("TRN2", target_bir_lowering=False)
    T = {}

    def din(name, shape, dt=F32):
        T[name] = nc.dram_tensor(name, shape, dt, kind="ExternalInput").ap()

    hasA, hasB, hasC, hasD = ("A" in phases), ("B" in phases), ("C" in phases), ("D" in phases)
    if hasA:
        din("xk", [L, D])
        din("w_in", [D, 6144])
        din("gmix_b", [128, D])
        din("qkg_b", [128, 2, 64])
        din("ck", [128, 33, 32])
        din("sk", [128, 33, 32])
        din("cq", [128, 16, 32])
        din("sq", [128, 16, 32])
    if hasA or hasC:
        din("xq", [TOWN, D])
    if hasC:
        din("w_out", [D, D])
    if hasD:
        din("w_up", [D, DFF])
        din("w_down", [DFF, D])
        din("gmlp_b", [128, D])
    if hasB:
        din("lam_b", [128, 4, 64])
        din("og_col", [128, 2])
        din("masks", [128, 4, 128])
    T["y"] = nc.dram_tensor("y", [TOWN, D], F32, kind="ExternalOutput").ap()
    skind = ("ExternalOutput" if debug else "Internal") if hasA else "ExternalInput"
    if hasA or hasB:
        for nm in ("kd", "ks"):
            T[nm] = nc.dram_tensor(nm, [8, 128, L], BF16, kind=skind).ap()
        for nm in ("vd", "vs"):
            T[nm] = nc.dram_tensor(nm, [8, 128, NB, 128], BF16, kind=skind).ap()
        for nm in ("qd", "qs"):
            T[nm] = nc.dram_tensor(nm, [8, 128, TOWN], BF16, kind=skind).ap()
    hkind = ("ExternalOutput" if debug else "Internal") if hasC else "ExternalInput"
    if hasC or hasD:
        T["h1"] = nc.dram_tensor("h1", [TOWN, D], F32, kind=hkind).ap()
    if debug and (hasB or hasC):
        mk = "ExternalOutput" if hasB else "ExternalInput"
        T["mixdump"] = nc.dram_tensor("mixdump", [128, 16, TOWN], BF16, kind=mk).ap()
    if debug and (hasC or hasD):
        rk = "ExternalOutput" if hasC else "ExternalInput"
        T["rstd2dump"] = nc.dram_tensor("rstd2dump", [128, 16], F32, kind=rk).ap()

    if hasA:
        phase_A(nc, T)
    with nc.sbuf_tensor("rstd2", [128, 16], F32) as rstd2:
        with nc.sbuf_tensor("mixT", [128, 16, TOWN], BF16) as mixT:
            if hasB:
                phase_B(nc, T, mixT, debug)
            if hasC:
                phase_C(nc, T, mixT, rstd2, debug, hasB)
        if hasD:
            phase_D(nc, T, rstd2, debug, hasC)
    return nc


def _rope_tables():
    pos = np.arange(L, dtype=np.float32) - np.float32(PAD)
    inv = (np.float32(10000.0) ** (-np.arange(0, 64, 2, dtype=np.float32) / np.float32(64))).astype(np.float32)
    ang = (pos[:, None] * inv[None, :]).astype(np.float32)
    return np.cos(ang).astype(np.float32), np.sin(ang).astype(np.float32)


def _prep_inputs(x, meta_tokens, g_mix, w_in, q_norm_g, k_norm_g, lambda_q1, lambda_k1, lambda_q2,
                 lambda_k2, diff_out_g, sb_out_g, w_out, g_mlp, w_up, w_down):
    f = np.float32
    x = np.asarray(x, f)
    B = x.shape[0]
    cos, sin = _rope_tables()
    ckt = np.ascontiguousarray(cos.reshape(NB, 128, 32).transpose(1, 0, 2))
    skt = np.ascontiguousarray(sin.reshape(NB, 128, 32).transpose(1, 0, 2))
    rep = lambda v, n=128: np.ascontiguousarray(np.broadcast_to(np.asarray(v, f).reshape(1, -1), (n, np.asarray(v).size)))
    gmix_b = rep(g_mix[0])
    gmlp_b = rep(g_mlp[0])
    qkg_b = np.ascontiguousarray(np.stack([rep(q_norm_g[0]), rep(k_norm_g[0])], axis=1))
    lam_b = np.ascontiguousarray(np.stack([rep(lambda_q1[0]), rep(lambda_k1[0]), rep(lambda_q2[0]),
                                           rep(lambda_k2[0])], axis=1))
    og_col = np.ascontiguousarray(np.stack([np.asarray(diff_out_g[0], f), np.asarray(sb_out_g[0], f)], axis=1))
    kk = np.arange(128)[:, None]
    qq = np.arange(128)[None, :]
    tri_le = (kk <= qq).astype(f)
    tri_lt = (kk < qq).astype(f)
    onesm = np.ones((128, 128), f)
    zerosm = np.zeros((128, 128), f)
    shared = {
        "w_in": np.ascontiguousarray(np.asarray(w_in[0], f)),
        "w_out": np.ascontiguousarray(np.asarray(w_out[0], f)),
        "w_up": np.ascontiguousarray(np.asarray(w_up[0], f)),
        "w_down": np.ascontiguousarray(np.asarray(w_down[0], f)),
        "gmix_b": gmix_b, "gmlp_b": gmlp_b, "qkg_b": qkg_b, "lam_b": lam_b, "og_col": og_col,
        "ck": ckt, "sk": skt,
    }
    in_maps = []
    own = []
    meta = np.asarray(meta_tokens, f)
    for c in range(8):
        b, p = c // 2, c % 2
        xk = np.concatenate([np.zeros((PAD, D), f), meta, x[b]], axis=0)
        blks = [2 * j + 1 + p for j in range(NOB)]
        own.append(blks)
        xq = np.ascontiguousarray(xk.reshape(NB, 128, D)[blks].reshape(TOWN, D))
        cqt = np.ascontiguousarray(ckt[:, blks, :])
        sqt = np.ascontiguousarray(skt[:, blks, :])
        if p == 0:
            masks = np.stack([tri_le, zerosm, tri_lt, zerosm], axis=1)
        else:
            masks = np.stack([onesm, tri_le, onesm, tri_lt], axis=1)
        m = dict(shared)
        m.update({"xk": np.ascontiguousarray(xk), "xq": xq, "cq": cqt, "sq": sqt,
                  "masks": np.ascontiguousarray(masks.astype(f))})
        in_maps.append(m)
    return in_maps, own, B


_NC_CACHE = {}


def kernel(x, meta_tokens, g_mix, w_in, q_norm_g, k_norm_g, lambda_q1, lambda_k1, lambda_q2, lambda_k2,
           diff_out_g, sb_out_g, w_out, g_mlp, w_up, w_down):
    in_maps, own, B = _prep_inputs(x, meta_tokens, g_mix, w_in, q_norm_g, k_norm_g, lambda_q1, lambda_k1,
                                   lambda_q2, lambda_k2, diff_out_g, sb_out_g, w_out, g_mlp, w_up, w_down)
    if "nc" not in _NC_CACHE:
        _NC_CACHE["nc"] = build_nc()
    nc = _NC_CACHE["nc"]
    res = run_bass_kernel_spmd(nc, in_maps, core_ids=list(range(8)))
    out = np.zeros((B, 4096, D), np.float32)
    for c in range(8):
        b = c // 2
        y = np.asarray(res.results[c]["y"], np.float32).reshape(NOB, 128, D)
        for j, blk in enumerate(own[c]):
            out[b, (blk - 1) * 128:blk * 128, :] = y[j]
    return out
```
